# Optimizing a Trainium2 kernel written in Bass

```python
import jax, jax.numpy as jnp
from jax import lax
import numpy as np

D_MODEL = 1024
BATCH = 8
SEQ = 2048
DEPTH = 2
DEC_BATCH = 128
DEC_SEQ = 4
PAST_LEN = 16384
PAGE_SIZE = 128

N_MIXERS = 2
N_A = (DEPTH + N_MIXERS - 1) // N_MIXERS
N_B = DEPTH // N_MIXERS
N_META = 16
D_RNN = D_MODEL
BLOCK_W = 256
N_BLK = D_RNN // BLOCK_W
CONV_A = 4
RG_C = 8.0
D_CONV = D_MODEL
CONV_B = 3
D_FF = ((8 * D_MODEL // 3 + 255) // 256) * 256
EPS = 1e-6

kernel_name = 'hybrid_rglru_shortconv_decode_step'


def _rmsnorm(x, w):
    xf = x.astype(jnp.float32)
    y = xf * lax.rsqrt(jnp.mean(xf * xf, axis=-1, keepdims=True) + EPS)
    return (y * w.astype(jnp.float32)).astype(x.dtype)


def _causal_dwconv(buf, u, w):
    K = w.shape[0]
    T = u.shape[1]
    full = jnp.concatenate([buf.astype(u.dtype), u], axis=1)
    y = full[:, 0:T] * w[0]
    for k in range(1, K):
        y = y + full[:, k:k + T] * w[k]
    return y, full[:, T:]


def _rglru(xc, h0, ga_w, ga_b, gx_w, gx_b, lam):
    Bn, T, _ = xc.shape
    xb = xc.reshape(Bn, T, N_BLK, BLOCK_W)
    r = jax.nn.sigmoid(jnp.einsum('bthi,hij->bthj', xb, ga_w) + ga_b).reshape(Bn, T, D_RNN)
    ig = jax.nn.sigmoid(jnp.einsum('bthi,hij->bthj', xb, gx_w) + gx_b).reshape(Bn, T, D_RNN)
    log_a = -RG_C * r.astype(jnp.float32) * jax.nn.softplus(-lam.astype(jnp.float32))
    a = jnp.exp(log_a)
    mult = jnp.sqrt(-jnp.expm1(2.0 * log_a))
    u = mult * (ig * xc).astype(jnp.float32)

    def step(h, au):
        a_t, u_t = au
        h = a_t * h + u_t
        return h, h

    hT, hs = lax.scan(step, h0.astype(jnp.float32), (jnp.swapaxes(a, 0, 1), jnp.swapaxes(u, 0, 1)))
    return jnp.swapaxes(hs, 0, 1).astype(xc.dtype), hT.astype(h0.dtype)


def _rglru_block(x, conv_buf, h0, w_in, conv_w, conv_b, ga_w, ga_b, gx_w, gx_b, lam, w_out):
    z = x @ w_in
    gate, xr = z[..., :D_RNN], z[..., D_RNN:]
    xc, new_buf = _causal_dwconv(conv_buf, xr, conv_w)
    xc = xc + conv_b
    hs, hT = _rglru(xc, h0, ga_w, ga_b, gx_w, gx_b, lam)
    y = (hs * jax.nn.gelu(gate)) @ w_out
    return y, new_buf, hT


def _shortconv_block(x, conv_buf, w_in, conv_w, w_out):
    z = x @ w_in
    bg, cg, v = z[..., :D_CONV], z[..., D_CONV:2 * D_CONV], z[..., 2 * D_CONV:]
    c, new_buf = _causal_dwconv(conv_buf, cg * v, conv_w)
    y = (bg * c) @ w_out
    return y, new_buf


def _swiglu(x, wg, wu, wd):
    return (jax.nn.silu(x @ wg) * (x @ wu)) @ wd


def setup_inputs(seed: int = 0) -> dict:
    key = jax.random.key(seed)
    ks = jax.random.split(key, 32)
    f32 = jnp.float32
    nrm = lambda k, shape, s: jax.random.normal(k, shape, f32) * s
    a_init = jax.random.uniform(ks[13], (N_A, D_RNN), f32, 0.9, 0.999)
    return {
        'x_prompt': nrm(ks[0], (BATCH, SEQ, D_MODEL), 1.0),
        'x_sample': nrm(ks[1], (DEC_BATCH, DEC_SEQ, D_MODEL), 1.0),
        'state_rglru_conv': nrm(ks[2], (N_A, DEC_BATCH, CONV_A - 1, D_RNN), 1.0),
        'state_rglru_h': nrm(ks[3], (N_A, DEC_BATCH, D_RNN), 0.5),
        'state_sconv': nrm(ks[4], (N_B, DEC_BATCH, CONV_B - 1, D_CONV), 1.0),
        'meta_tokens': nrm(ks[5], (N_META, D_MODEL), 1.0),
        'norm_mix_pre': 1.0 + nrm(ks[6], (DEPTH, D_MODEL), 0.05),
        'norm_mix_post': 1.0 + nrm(ks[7], (DEPTH, D_MODEL), 0.05),
        'norm_ffn_pre': 1.0 + nrm(ks[8], (DEPTH, D_MODEL), 0.05),
        'norm_ffn_post': 1.0 + nrm(ks[9], (DEPTH, D_MODEL), 0.05),
        'rg_w_in': nrm(ks[10], (N_A, D_MODEL, 2 * D_RNN), D_MODEL ** -0.5),
        'rg_conv_w': nrm(ks[11], (N_A, CONV_A, D_RNN), CONV_A ** -0.5),
        'rg_conv_b': nrm(ks[12], (N_A, D_RNN), 0.02),
        'rg_gate_a_w': nrm(ks[14], (N_A, N_BLK, BLOCK_W, BLOCK_W), BLOCK_W ** -0.5),
        'rg_gate_a_b': nrm(ks[15], (N_A, N_BLK, BLOCK_W), 0.02),
        'rg_gate_x_w': nrm(ks[16], (N_A, N_BLK, BLOCK_W, BLOCK_W), BLOCK_W ** -0.5),
        'rg_gate_x_b': nrm(ks[17], (N_A, N_BLK, BLOCK_W), 0.02),
        'rg_lambda': jnp.log(a_init) - jnp.log1p(-a_init),
        'rg_w_out': nrm(ks[18], (N_A, D_RNN, D_MODEL), D_RNN ** -0.5),
        'sc_w_in': nrm(ks[19], (N_B, D_MODEL, 3 * D_CONV), D_MODEL ** -0.5),
        'sc_conv_w': nrm(ks[20], (N_B, CONV_B, D_CONV), CONV_B ** -0.5),
        'sc_w_out': nrm(ks[21], (N_B, D_CONV, D_MODEL), D_CONV ** -0.5),
        'ffn_w_gate': nrm(ks[22], (DEPTH, D_MODEL, D_FF), D_MODEL ** -0.5),
        'ffn_w_up': nrm(ks[23], (DEPTH, D_MODEL, D_FF), D_MODEL ** -0.5),
        'ffn_w_down': nrm(ks[24], (DEPTH, D_FF, D_MODEL), D_FF ** -0.5),
    }


def reference(x_prompt, x_sample, state_rglru_conv, state_rglru_h, state_sconv, meta_tokens,
              norm_mix_pre, norm_mix_post, norm_ffn_pre, norm_ffn_post,
              rg_w_in, rg_conv_w, rg_conv_b, rg_gate_a_w, rg_gate_a_b, rg_gate_x_w, rg_gate_x_b,
              rg_lambda, rg_w_out, sc_w_in, sc_conv_w, sc_w_out,
              ffn_w_gate, ffn_w_up, ffn_w_down):

    def run_trunk(x, rg_conv_in, rg_h_in, sc_in):
        rg_conv_out, rg_h_out, sc_out = [], [], []
        for i in range(DEPTH):
            j = i // N_MIXERS
            hn = _rmsnorm(x, norm_mix_pre[i])
            if i % N_MIXERS == 0:
                m, cb, hT = _rglru_block(hn, rg_conv_in[j], rg_h_in[j], rg_w_in[j], rg_conv_w[j], rg_conv_b[j],
                                         rg_gate_a_w[j], rg_gate_a_b[j], rg_gate_x_w[j], rg_gate_x_b[j],
                                         rg_lambda[j], rg_w_out[j])
                rg_conv_out.append(cb)
                rg_h_out.append(hT)
            else:
                m, cb = _shortconv_block(hn, sc_in[j], sc_w_in[j], sc_conv_w[j], sc_w_out[j])
                sc_out.append(cb)
            x = x + _rmsnorm(m, norm_mix_post[i])
            f = _swiglu(_rmsnorm(x, norm_ffn_pre[i]), ffn_w_gate[i], ffn_w_up[i], ffn_w_down[i])
            x = x + _rmsnorm(f, norm_ffn_post[i])
        return x, jnp.stack(rg_conv_out), jnp.stack(rg_h_out), jnp.stack(sc_out)

    dt = x_prompt.dtype
    meta = jnp.broadcast_to(meta_tokens[None].astype(dt), (BATCH, N_META, D_MODEL))
    xp = jnp.concatenate([meta, x_prompt], axis=1)
    zc_a = jnp.zeros((N_A, BATCH, CONV_A - 1, D_RNN), dt)
    zh_a = jnp.zeros((N_A, BATCH, D_RNN), dt)
    zc_b = jnp.zeros((N_B, BATCH, CONV_B - 1, D_CONV), dt)
    yp, rg_conv_p, rg_h_p, sc_p = run_trunk(xp, zc_a, zh_a, zc_b)
    y_prompt = yp[:, N_META:]

    y_sample, rg_conv_s, rg_h_s, sc_s = run_trunk(x_sample, state_rglru_conv, state_rglru_h, state_sconv)

    return (y_prompt, y_sample, rg_conv_p, rg_h_p, sc_p, rg_conv_s, rg_h_s, sc_s)
```

```python
import numpy as np
import concourse.bass as bass
import concourse.mybir as mybir
from concourse.bass_utils import run_bass_kernel_spmd

F32 = mybir.dt.float32
BF16 = mybir.dt.bfloat16
AF = mybir.ActivationFunctionType
ALU = mybir.AluOpType

D = 1024
KC = 8
DFF = 2816
FC = 22
NMETA = 16
EPS = 1e-6
NS = 16
ST = 4
NSC = NS * ST
N_CORES = 8

R_CONV, R_H, R_SC, R_NORM, R_RGCW, R_RGCB, R_GAB, R_GXB, R_LAM, R_SCW = 0, 48, 64, 96, 104, 108, 109, 110, 111, 112
NPST = 115
O_RGC_P, O_RGH_P, O_SC_P, O_RGC_S, O_RGH_S, O_SC_S = 0, 3, 4, 6, 54, 70
NOUTST = 102
LNHALF = -0.6931471805599453

ENGS = ("pe", "act", "dve", "pool", "sp")
NSLOT = 6
SLOT_ELEMS = 4096
N_DSEM = 16


class Sched:
    def __init__(self):
        self.q = {e: [] for e in ENGS}
        self.cnt = {e: 0 for e in ENGS}
        self.seen = {e: {} for e in ENGS}
        self.res = {}
        self.dval = [0] * N_DSEM

    def _deps(self, eng, reads, writes):
        deps = {}

        def need(k, v):
            if k == "pe" and eng == "pe":
                return
            if deps.get(k, 0) < v:
                deps[k] = v

        for r in reads:
            e = self.res.get(r)
            if e and e[0]:
                need(*e[0])
        for w in writes:
            e = self.res.get(w)
            if e:
                if e[0]:
                    need(*e[0])
                for k, v in e[1].items():
                    need(k, v)
        out = []
        for k, v in deps.items():
            if self.seen[eng].get(k, 0) < v:
                self.seen[eng][k] = v
                out.append((k, v))
        return out

    def _record(self, tok, reads, writes):
        for r in reads:
            e = self.res.setdefault(r, [None, {}])
            if e[1].get(tok[0], 0) < tok[1]:
                e[1][tok[0]] = tok[1]
        for w in writes:
            self.res[w] = [tok, {}]

    def op(self, eng, fn, reads=(), writes=(), inc=True):
        waits = self._deps(eng, reads, writes)
        tok = (eng, self.cnt[eng] + 1)
        if inc:
            self.cnt[eng] += 1
        self._record(tok, reads, writes)
        self.q[eng].append((waits, fn, eng if inc else None, False))
        return tok

    def dma(self, eng, fn, dsem, reads=(), writes=(), final=None):
        waits = self._deps(eng, reads, writes)
        self.dval[dsem] += 16
        tok = (("d", dsem), final if final is not None else self.dval[dsem])
        self._record(tok, reads, writes)
        self.q[eng].append((waits, fn, ("d", dsem), True))
        return tok

    def barrier(self, engs=("pe", "act", "dve")):
        for e in engs:
            waits = []
            for o in engs:
                if (o != e or e != "pe") and self.seen[e].get(o, 0) < self.cnt[o]:
                    self.seen[e][o] = self.cnt[o]
                    waits.append((o, self.cnt[o]))
            if waits:
                self.q[e].append((waits, None, None, False))

    def final_wait(self, eng):
        waits = []
        for e in ("pe", "act", "dve", "pool"):
            if e != eng and self.cnt[e] > 0:
                waits.append((e, self.cnt[e]))
        for i, v in enumerate(self.dval):
            if v > 0:
                waits.append((("d", i), v))
        self.q[eng].append((waits, None, None, False))

    def emit(self, name, eng, esem, dsems):
        def sem_of(k):
            return dsems[k[1]] if isinstance(k, tuple) else esem[k]

        for waits, fn, upd, is_dma in self.q[name]:
            sems = [(sem_of(k), v) for k, v in waits]
            if fn is None:
                for s, v in sems:
                    eng.wait_ge(s, v)
                continue
            attach = None
            if sems and not is_dma:
                attach = sems[0]
                sems = sems[1:]
            for s, v in sems:
                eng.wait_ge(s, v)
            ins = fn(eng)
            if attach is not None:
                ins._wait_ge(attach[0], attach[1])
            if upd is not None:
                if is_dma:
                    ins.then_inc(dsems[upd[1]], 16)
                else:
                    ins.then_inc(esem[upd], 1)


def split_tiles(n, maxw=512):
    k = (n + maxw - 1) // maxw
    if n > 256 and k < 2:
        k = 2
    base, rem = divmod(n, k)
    out, o = [], 0
    for i in range(k):
        s = base + (1 if i < rem else 0)
        out.append((o, s))
        o += s
    return out


def make_groups(T, ng):
    base, rem = divmod(T, ng)
    out, o = [], 0
    for i in range(ng):
        s = base + (1 if i < rem else 0)
        out.append((o, o + s))
        o += s
    return out


def build(TP, NG=3, skip=(), dbg=99):
    T = TP + NSC
    groups = make_groups(T, NG)
    TG = max(c1 - c0 for c0, c1 in groups)
    assert groups[-1][0] < TP - 3 and all(c1 <= TP for c0, c1 in groups[:-1]) and all(min(c1, TP) - c0 >= 3 for c0, c1 in groups)
    NTW = max(s for c0, c1 in groups for o, s in split_tiles(c1 - c0))

    nc = bass.Bass("TRN2", target_bir_lowering=False)
    dr = lambda n, s: nc.dram_tensor(n, s, F32, kind="ExternalInput").ap()
    xin = dr("xin", [T, D])
    pst = dr("pst", [NPST, D])
    ident_d = dr("ident", [128, 128])
    rg_w_in = dr("rg_w_in", [D, 2 * D])
    ga_w = dr("ga_w", [4, 256, 256])
    gx_w = dr("gx_w", [4, 256, 256])
    rg_w_out = dr("rg_w_out", [D, D])
    sc_w_in = dr("sc_w_in", [D, 3 * D])
    sc_w_out = dr("sc_w_out", [D, D])
    w_gate = dr("w_gate", [2, D, DFF])
    w_up = dr("w_up", [2, D, DFF])
    w_down = dr("w_down", [2, DFF, D])
    yout = nc.dram_tensor("yout", [T + NOUTST, D], F32, kind="ExternalOutput").ap()

    from contextlib import ExitStack
    S = Sched()
    with ExitStack() as es:
        sb = lambda n, s, dt=F32: es.enter_context(nc.sbuf_tensor(n, s, dt))
        X = sb("X", [128, KC, TG])
        HN = sb("HN", [128, KC, TG], BF16)
        G = sb("G", [128, KC, TG], BF16)
        SQ = sb("SQ", [128, KC, TG], BF16)
        Y = sb("Y", [128, KC, TG])
        RSTD = sb("RSTD", [128, TG])
        ARENA_B = max(FC * TG * 2, 2 * ((2 * (NTW + 3) + 2 * NS * 7 + 10 * NTW) * 4 + (2 * NTW + 2) * 2) + 64)
        ARENA_B = (ARENA_B + 63) // 64 * 64
        AR = sb("AR", [128, ARENA_B // 4])
        WR = sb("WR", [128, NSLOT, SLOT_ELEMS], BF16)
        GW = sb("GW", [128, 2, 4, 2, 256], BF16)
        INB = sb("INB", [128, 2, D])
        OUTB = sb("OUTB", [128, 2, D])
        PST = sb("PST", [128, KC, 128])
        OST = sb("OST", [128, KC, 128])
        IDENT = sb("IDENT", [128, 128])
        ONES = sb("ONES", [128, 128], BF16)
        CONVST = sb("CONVST", [128, KC, 4])
        SCST = sb("SCST", [128, KC, 2])
        HST = sb("HST", [128, KC])
        CNEG = sb("CNEG", [128, KC])
        C2NEG = sb("C2NEG", [128, KC])
        TMPV = sb("TMPV", [128, KC])
        HC = sb("HC", [128, KC])
        HGB = sb("HGB", [128, 2, KC])
        TMP16 = sb("TMP16", [128, 2, NS])
        SIL = sb("SIL", [128, 2, NTW])
        PS = es.enter_context(nc.psum_tensor("PS", [128, 8, 512], F32))
        esem = {e: es.enter_context(nc.semaphore("s_" + e)) for e in ENGS}
        dsems = [es.enter_context(nc.semaphore("d%d" % i)) for i in range(N_DSEM)]
        block = es.enter_context(nc.Block())

        arena_f32 = AR[:, :]
        arena_bf = AR[:, :].bitcast(BF16)
        ACTB = arena_bf[:, 0:FC * TG].rearrange("p (j t) -> p j t", t=TG)
        off = [0]

        def carve(nelem, dt=F32, shape=None):
            o = off[0]
            if dt == F32:
                v = arena_f32[:, o:o + nelem]
                off[0] += nelem
            else:
                v = arena_bf[:, 2 * o:2 * o + nelem]
                off[0] += (nelem + 1) // 2
            return v

        W_ = NTW
        off[0] = 0
        RGS = []
        for i in range(2):
            d = {}
            d["XR"] = carve(2 * (W_ + 3)).rearrange("p (j t) -> p j t", t=W_ + 3)
            d["XRS"] = carve(2 * NS * 7).rearrange("p (j s t) -> p j s t", s=NS, t=7)
            for nm in ("GG", "XC", "RR", "IG", "AA"):
                d[nm] = carve(2 * W_).rearrange("p (j t) -> p j t", t=W_)
            d["HH"] = d["RR"]
            d["XCB"] = carve(2 * W_ + 2, BF16)[:, 0:2 * W_].rearrange("p (j t) -> p j t", t=W_)
            RGS.append(d)
        rg_words = off[0]
        off[0] = 0
        SCS = []
        for i in range(2):
            d = {}
            d["BG"] = carve(W_)
            d["CV"] = carve(W_ + 2)
            d["CS"] = carve(NS * 6).rearrange("p (s t) -> p s t", t=6)
            d["CC"] = carve(W_)
            d["CG"] = carve(W_)
            SCS.append(d)
        assert max(rg_words, off[0]) * 4 <= ARENA_B, (rg_words, off[0], ARENA_B)

        def OP(eng, method, *args, reads=(), writes=(), inc=True, **kw):
            return S.op(eng, lambda e: getattr(e, method)(*args, **kw), reads, writes, inc)

        def DMA(eng, dsem, out, in_, reads=(), writes=(), final=None):
            return S.dma(eng, lambda e: e.dma_start(out=out, in_=in_), dsem, reads, writes, final)

        def ACTF(out, in_, func, reads, writes, **kw):
            return OP("act", "activation", out=out, in_=in_, func=func, reads=reads, writes=writes, **kw)

        bank_i = [0]

        def next_bank():
            b = bank_i[0] % 8
            bank_i[0] += 1
            return b

        def mm_group(out_ap, pairs, reads_of, bank):
            n = len(pairs)
            for i, (lhsT, rhs) in enumerate(pairs):
                OP("pe", "matmul", out_ap, lhsT, rhs, start=(i == 0), stop=(i == n - 1),
                   reads=reads_of(i), writes=[("PS", bank)], inc=(i == n - 1))

        def plan_weights():
            plan = []
            for g in range(NG):
                nnt = len(split_tiles(groups[g][1] - groups[g][0]))
                for b in range(4 if ("l0mix" not in skip and dbg >= 11) else 0):
                    plan.append((8, 512, [(rg_w_in, 256 * b, 256, 0), (rg_w_in, D + 256 * b, 256, 256)]))
                for q in range(2 if ("l0mix" not in skip and dbg >= 15) else 0):
                    plan.append((8, 512, [(rg_w_out, 512 * q, 512, 0)]))
                for L in range(2):
                    if L == 1 and "l1mix" not in skip:
                        for c in range(KC):
                            plan.append((8, 384, [(sc_w_in, 128 * c, 128, 0), (sc_w_in, D + 128 * c, 128, 128),
                                                  (sc_w_in, 2 * D + 128 * c, 128, 256)]))
                        for q in range(2):
                            plan.append((8, 512, [(sc_w_out, 512 * q, 512, 0)]))
                    if ("ffn%d" % L) in skip:
                        continue
                    for q in range(6):
                        w = min(512, DFF - 512 * q)
                        plan.append((8, w, [(w_gate[L], 512 * q, w, 0)]))
                        plan.append((8, w, [(w_up[L], 512 * q, w, 0)]))
                    for n_ in range(nnt):
                        for m in range(KC):
                            plan.append((FC, 128, [(w_down[L], 128 * m, 128, 0)]))
            return plan

        wstate = {"next_use": 0, "issued": 0, "plan": plan_weights()}

        def wslot_view(u):
            K, width, parts = wstate["plan"][u]
            s = u % NSLOT
            return WR[:, s, 0:K * width].rearrange("p (k n) -> p k n", n=width)

        def issue_weight(u):
            K, width, parts = wstate["plan"][u]
            s = u % NSLOT
            view = wslot_view(u)
            fin = S.dval[s] + 16 * len(parts)
            for pi, (w, c0, wd, doff) in enumerate(parts):
                src = w.rearrange("(k p) n -> p k n", p=128)[:, :, c0:c0 + wd]
                DMA("pool", s, view[:, :, doff:doff + wd], src, writes=[("W", s, pi)], final=fin)

        def next_weight(pending=0):
            u = wstate["next_use"]
            wstate["next_use"] += 1
            while wstate["issued"] < min(len(wstate["plan"]), u + NSLOT - pending):
                issue_weight(wstate["issued"])
                wstate["issued"] += 1
            s = u % NSLOT
            return wslot_view(u), [("W", s, 0), ("W", s, 1), ("W", s, 2)]

        DMA("sp", 12, IDENT[:, :], ident_d[:, :], writes=["IDENT"])
        DMA("sp", 13, INB[0:NPST, 0, :], pst[:, :], writes=[("INB", 0)])
        OP("dve", "memset", ONES[:, :], 1.0, writes=["ONES"])
        OP("dve", "memset", CONVST[:, :, :], 0.0, writes=[("CONVST", c) for c in range(KC)])
        OP("dve", "memset", SCST[:, :, :], 0.0, writes=[("SCST", c) for c in range(KC)])
        OP("dve", "memset", HST[:, :], 0.0, writes=[("HST", c) for c in range(KC)])
        OP("dve", "memset", OST[:, :, :], 0.0, writes=[("OST", c) for c in range(KC)])
        for t, gwd in enumerate((ga_w, gx_w) if dbg >= 2 else ()):
            for b in range(4):
                DMA("pool", 6 + t, GW[:, t, b, :, :], gwd[b].rearrange("(ki p) n -> p ki n", p=128), writes=[("GW", t, b)], final=64)
        GWK = [[("GW", t, b) for b in range(4)] for t in range(2)]

        def transpose_in(slot, rows, dst_of_half, keys_of_half, alt=0):
            for half in range(2):
                b = next_bank()
                for q in range(4):
                    c = half * 4 + q
                    OP("pe", "transpose", PS[:, b, q * 128:q * 128 + rows], INB[0:rows, slot, c * 128:(c + 1) * 128],
                       IDENT[0:rows, 0:rows], reads=[("INB", slot), "IDENT"], writes=[("PS", b)], inc=(q == 3))
                src = PS[:, b, :].rearrange("p (q r) -> p q r", r=128)[:, :, 0:rows]
                if (half + alt) % 2 == 0:
                    OP("dve", "tensor_copy", dst_of_half(half), src, reads=[("PS", b)], writes=keys_of_half(half))
                else:
                    ACTF(dst_of_half(half), src, AF.Copy, [("PS", b)], keys_of_half(half))

        if dbg >= 3:
            transpose_in(0, NPST, lambda h: PST[:, h * 4:(h + 1) * 4, 0:NPST], lambda h: [("PST", h)])
        PSTK = [("PST", 0), ("PST", 1)]
        if dbg >= 4:
            ACTF(TMPV[:, :], PST[:, :, R_LAM], AF.Exp, PSTK, ["TMPV"], scale=-1.0)
            ACTF(TMPV[:, :], TMPV[:, :], AF.Ln, ["TMPV"], ["TMPV"], bias=1.0)
            OP("dve", "tensor_scalar", CNEG[:, :], TMPV[:, :], -8.0, None, op0=ALU.mult, reads=["TMPV"], writes=["CNEG"])
            OP("dve", "tensor_scalar", C2NEG[:, :], TMPV[:, :], -16.0, None, op0=ALU.mult, reads=["TMPV"], writes=["C2NEG"])
            OP("dve", "tensor_scalar", HC[:, :], TMPV[:, :], -4.0, None, op0=ALU.mult, reads=["TMPV"], writes=["HC"])
            OP("dve", "tensor_scalar", HGB[:, 0, :], PST[:, :, R_GAB], 0.5, None, op0=ALU.mult, reads=PSTK, writes=["HGB0"])
            OP("dve", "tensor_scalar", HGB[:, 1, :], PST[:, :, R_GXB], 0.5, None, op0=ALU.mult, reads=PSTK, writes=["HGB1"])

        def pcol(c, r):
            return PST[:, c, r:r + 1]

        in_slot = [1]
        out_slot = [0]

        def norm_rstd(n, o, sz):
            b = next_bank()
            mm_group(PS[:, b, 0:sz], [(ONES[:, :], SQ[:, c, o:o + sz]) for c in range(KC)],
                     lambda i: [("SQ", i, n), "ONES"], b)
            ACTF(RSTD[:, o:o + sz], PS[:, b, 0:sz], AF.Ln, [("PS", b)], [("RSTD", n)], scale=1.0 / D, bias=EPS)
            ACTF(RSTD[:, o:o + sz], RSTD[:, o:o + sz], AF.Exp, [("RSTD", n)], [("RSTD", n)], scale=-0.5)

        def prenorm_n(n, o, sz, nrow):
            ACTF(SQ[:, :, o:o + sz], X[:, :, o:o + sz], AF.Square,
                 [("X", c, n) for c in range(KC)], [("SQ", c, n) for c in range(KC)])
            norm_rstd(n, o, sz)
            for c in range(KC):
                OP("dve", "scalar_tensor_tensor", HN[:, c, o:o + sz], X[:, c, o:o + sz], pcol(c, nrow), RSTD[:, o:o + sz],
                   op0=ALU.mult, op1=ALU.mult, reads=[("X", c, n), ("RSTD", n)] + PSTK, writes=[("HN", c, n)])

        def prenorm(NT, nrow):
            for n, (o, sz) in enumerate(NT):
                prenorm_n(n, o, sz, nrow)

        def postnorm_n(n, o, sz, nrow):
            norm_rstd(n, o, sz)
            for c in range(KC):
                OP("dve", "scalar_tensor_tensor", Y[:, c, o:o + sz], Y[:, c, o:o + sz], pcol(c, nrow), RSTD[:, o:o + sz],
                   op0=ALU.mult, op1=ALU.mult, reads=[("Y", c, n), ("RSTD", n)] + PSTK, writes=[("Y", c, n)])
                OP("dve", "tensor_tensor", X[:, c, o:o + sz], X[:, c, o:o + sz], Y[:, c, o:o + sz], op=ALU.add,
                   reads=[("X", c, n), ("Y", c, n)], writes=[("X", c, n)])

        def proj_out(NT, src, srckey, nk, wview, post_row, next_pre_row, mh, hooks=None, after_all=None):
            for n, (o, sz) in enumerate(NT):
                for m in range(KC):
                    wv, wkeys, mo = wview(m, n)
                    b = next_bank()
                    mm_group(PS[:, b, 0:sz], [(wv[:, k, mo:mo + 128], src[:, k, o:o + sz]) for k in range(nk)],
                             lambda i: wkeys + [(srckey, i, n)], b)
                    ACTF(Y[:, m, o:o + sz], PS[:, b, 0:sz], AF.Copy, [("PS", b)], [("Y", m, n)])
                    ACTF(SQ[:, m, o:o + sz], PS[:, b, 0:sz], AF.Square, [("PS", b)], [("SQ", m, n)])
                    if next_pre_row is not None and n > 0 and m == mh:
                        po, psz = NT[n - 1]
                        prenorm_n(n - 1, po, psz, next_pre_row)
                    if hooks and (n, m) in hooks:
                        hooks[(n, m)]()
                postnorm_n(n, o, sz, post_row)
            if after_all:
                after_all()
            if next_pre_row is not None:
                po, psz = NT[-1]
                return [lambda: prenorm_n(len(NT) - 1, po, psz, next_pre_row)]
            return []

        def mixer_out(NT, post_row, next_pre_row, hooks=None):
            v0, k0 = next_weight()
            v1, k1 = next_weight(pending=1)
            views = [(v0, k0), (v1, k1)]
            return proj_out(NT, G, "G", KC, lambda m, n: (views[m // 4][0], views[m // 4][1], (m % 4) * 128), post_row, next_pre_row, 6, hooks)

        def ffn(NT, L, next_pre_row, pend, hooks=None, after_all=None):
            if ("ffn%d" % L) in skip:
                if after_all:
                    after_all()
                return pend
            for qs in ([0, 1], [2], [3], [4], [5]):
                slots = {}
                for i, q in enumerate(qs):
                    slots[q] = (next_weight(pending=2 * i), next_weight(pending=2 * i + 1))
                for n, (o, sz) in enumerate(NT):
                    if n == len(NT) - 1 and pend:
                        pend.pop()()
                    for q in qs:
                        w = min(512, DFF - 512 * q)
                        (wg, kg), (wu, ku) = slots[q]
                        for jj in range(w // 128):
                            j = 4 * q + jj
                            bg_ = next_bank()
                            bu_ = next_bank()
                            for (bb, wv, wk) in ((bg_, wg, kg), (bu_, wu, ku)):
                                mm_group(PS[:, bb, 0:sz], [(wv[:, k, jj * 128:(jj + 1) * 128], HN[:, k, o:o + sz]) for k in range(KC)],
                                         lambda i, wk=wk: wk + [("HN", i, n)], bb)
                            si = (j * len(NT) + n) % 2
                            ACTF(SIL[:, si, 0:sz], PS[:, bg_, 0:sz], AF.Silu, [("PS", bg_)], [("SIL", si)])
                            OP("dve", "tensor_tensor", ACTB[:, j, o:o + sz], SIL[:, si, 0:sz], PS[:, bu_, 0:sz], op=ALU.mult,
                               reads=[("SIL", si), ("PS", bu_)], writes=[("ACTB", j, n)])

            def wview(m, n):
                v, k = next_weight()
                return v, k, 0

            return proj_out(NT, ACTB, "ACTB", FC, wview, R_NORM + 2 * 3 + L, next_pre_row, 3, hooks, after_all)

        def store_rows(src_of_c, rows, drow0, keys_of_c):
            slot = out_slot[0] % 2
            out_slot[0] += 1
            for half in range(2):
                b = next_bank()
                for q in range(4):
                    c = half * 4 + q
                    OP("pe", "transpose", PS[0:rows, b, q * 128:(q + 1) * 128], src_of_c(c), IDENT[:, :],
                       reads=["IDENT"] + keys_of_c(c), writes=[("PS", b)], inc=(q == 3))
                if half == 0:
                    OP("dve", "tensor_copy", OUTB[0:rows, slot, 0:512], PS[0:rows, b, :], reads=[("PS", b)], writes=[("OUTB", slot, 0)])
                else:
                    ACTF(OUTB[0:rows, slot, 512:1024], PS[0:rows, b, :], AF.Copy, [("PS", b)], [("OUTB", slot, 1)])
            DMA("sp", 8 + slot, yout[drow0:drow0 + rows, :], OUTB[0:rows, slot, :], reads=[("OUTB", slot, 0), ("OUTB", slot, 1)])

        NTs = [split_tiles(c1_ - c0_) for (c0_, c1_) in groups]

        def emit_load(g, n):
            c0g = groups[g][0]
            o, sz = NTs[g][n]
            for lo in range(o, o + sz, 128):
                rows = min(128, o + sz - lo)
                slot = in_slot[0] % 2
                in_slot[0] += 1
                DMA("sp", 10 + slot, INB[0:rows, slot, :], xin[c0g + lo:c0g + lo + rows, :], writes=[("INB", slot)])
                transpose_in(slot, rows, lambda h: X[:, h * 4:(h + 1) * 4, lo:lo + rows],
                             lambda h: [("X", c, n) for c in range(4 * h, 4 * h + 4)], alt=(lo // 128))

        def emit_store(g, n):
            c0g = groups[g][0]
            o, sz = NTs[g][n]
            for lo in range(o, o + sz, 128):
                rows = min(128, o + sz - lo)
                store_rows(lambda c: X[:, c, lo:lo + rows], rows, c0g + lo, lambda c: [("X", c, n)])

        def emit_l0_pre(g, n):
            if "l0mix" not in skip:
                prenorm_n(n, NTs[g][n][0], NTs[g][n][1], R_NORM + 0)

        def group_tail_hooks(g):
            L_ = len(NTs[g])
            nxt = g + 1 if g + 1 < len(groups) else None
            Ln = len(NTs[nxt]) if nxt is not None else 0
            hooks = {}
            for n in range(1, L_):
                hooks[(n, 3)] = (lambda n=n: emit_store(g, n - 1))
                if nxt is not None and n - 1 < Ln:
                    hooks[(n, 5)] = (lambda n=n: emit_load(nxt, n - 1))
                    if n - 1 < Ln - 1:
                        hooks[(n, 6)] = (lambda n=n: emit_l0_pre(nxt, n - 1))

            def after_all():
                emit_store(g, L_ - 1)
                if nxt is not None:
                    for n2 in range(L_ - 1, Ln):
                        emit_load(nxt, n2)
                        if n2 < Ln - 1:
                            emit_l0_pre(nxt, n2)
            return hooks, after_all

        for g, (c0, c1) in enumerate(groups if dbg >= 5 else ()):
            Tg = c1 - c0
            Tp = min(c1, TP) - c0
            has_s = c1 > TP
            so = Tp
            last_p = (min(c1, TP) == TP)
            NT = split_tiles(Tg)
            allN = list(range(len(NT)))

            if g == 0:
                for n_ in allN:
                    emit_load(0, n_)
                    if n_ < len(NT) - 1:
                        emit_l0_pre(0, n_)

            def tile_info(n):
                o, sz = NT[n]
                tp = max(0, min(Tp - o, sz))
                assert tp == 0 or tp >= 3
                hs = has_s and (o <= so < o + sz)
                if has_s:
                    assert (o <= so and so + NSC <= o + sz) or (so + NSC <= o or so >= o + sz)
                return o, sz, tp, hs, (so - o), (last_p and tp > 0 and o + tp == Tp)

            pend = []
            if "l0mix" not in skip:
                pend = [lambda: prenorm_n(len(NT) - 1, NT[-1][0], NT[-1][1], R_NORM + 0)]
            unit_i = [0]

            def rg_unit(blk, n, wv, wkeys):
                o, sz, tp, hs, sl, lastp_here = tile_info(n)
                B_ = RGS[unit_i[0] % 2]
                bsx = unit_i[0] % 2
                unit_i[0] += 1
                XR, XRS, GG, XC, RR, IG, AA, HH, XCB = (B_[k] for k in ("XR", "XRS", "GG", "XC", "RR", "IG", "AA", "HH", "XCB"))
                K_ = lambda nm, j: (nm, bsx, j)
                for mi in (2, 3, 0, 1):
                    j = mi % 2
                    b = next_bank()
                    mm_group(PS[:, b, 0:sz], [(wv[:, k, mi * 128:(mi + 1) * 128], HN[:, k, o:o + sz]) for k in range(KC)],
                             lambda i: wkeys + [("HN", i, n)], b)
                    if mi >= 2:
                        OP("dve", "tensor_copy", XR[:, j, 3:3 + sz], PS[:, b, 0:sz], reads=[("PS", b)], writes=[K_("XR", j)])
                    else:
                        ACTF(GG[:, j, 0:sz], PS[:, b, 0:sz], AF.Gelu_apprx_tanh, [("PS", b)], [K_("GG", j)])
                for j in range(2):
                    c = 2 * blk + j
                    xrk = [K_("XR", j)]
                    OP("dve", "tensor_copy", XR[:, j, 0:3], CONVST[:, c, 0:3], reads=[("CONVST", c)], writes=[K_("XRpre", j)])
                    if tp > 0:
                        OP("dve", "tensor_scalar", XC[:, j, 0:tp], XR[:, j, 3:3 + tp], pcol(c, R_RGCW + 3), pcol(c, R_RGCB),
                           op0=ALU.mult, op1=ALU.add, reads=xrk + PSTK, writes=[K_("XC", j)])
                        for k in range(3):
                            OP("dve", "scalar_tensor_tensor", XC[:, j, 0:tp], XR[:, j, k:k + tp], pcol(c, R_RGCW + k), XC[:, j, 0:tp],
                               op0=ALU.mult, op1=ALU.add, reads=xrk + [K_("XRpre", j), K_("XC", j)] + PSTK, writes=[K_("XC", j)])
                        OP("dve", "tensor_copy", CONVST[:, c, 0:3], XR[:, j, tp:tp + 3], reads=xrk + [K_("XRpre", j)], writes=[("CONVST", c)])
                        if lastp_here:
                            OP("dve", "tensor_copy", OST[:, c, O_RGC_P:O_RGC_P + 3], XR[:, j, tp:tp + 3],
                               reads=xrk + [K_("XRpre", j)], writes=[("OST", c)])
                    if hs:
                        OP("dve", "tensor_copy", XRS[:, j, :, 0:3], PST[:, c, R_CONV:R_CONV + 48].rearrange("p (s t) -> p s t", t=3),
                           reads=PSTK, writes=[K_("XRS", j)])
                        OP("dve", "tensor_copy", XRS[:, j, :, 3:7], XR[:, j, 3 + sl:3 + sl + NSC].rearrange("p (s t) -> p s t", t=ST),
                           reads=xrk + [K_("XRS", j)], writes=[K_("XRS", j)])
                        xcs = XC[:, j, sl:sl + NSC].rearrange("p (s t) -> p s t", t=ST)
                        OP("dve", "tensor_scalar", xcs, XRS[:, j, :, 3:7], pcol(c, R_RGCW + 3), pcol(c, R_RGCB),
                           op0=ALU.mult, op1=ALU.add, reads=[K_("XRS", j)] + PSTK, writes=[K_("XC", j)])
                        for k in range(3):
                            OP("dve", "scalar_tensor_tensor", xcs, XRS[:, j, :, k:k + ST], pcol(c, R_RGCW + k), xcs,
                               op0=ALU.mult, op1=ALU.add, reads=[K_("XRS", j), K_("XC", j)] + PSTK, writes=[K_("XC", j)])
                        OP("dve", "tensor_copy", OST[:, c, O_RGC_S:O_RGC_S + 48].rearrange("p (s t) -> p s t", t=3), XRS[:, j, :, 4:7],
                           reads=[K_("XRS", j)], writes=[("OST", c)])

                def fb():
                    ACTF(XCB[:, :, 0:sz], XC[:, :, 0:sz], AF.Copy, [K_("XC", 0), K_("XC", 1)], [K_("XCB", 0), K_("XCB", 1)])

                def ba():
                    for t in range(2):
                        for mo in range(2):
                            c = 2 * blk + mo
                            b = next_bank()
                            mm_group(PS[:, b, 0:sz], [(GW[:, t, blk, ki, mo * 128:(mo + 1) * 128], XCB[:, ki, 0:sz]) for ki in range(2)],
                                     lambda i: GWK[t] + [K_("XCB", i)], b)
                            dstb = RR if t == 0 else IG
                            ACTF(dstb[:, mo, 0:sz], PS[:, b, 0:sz], AF.Tanh, [("PS", b), "HGB%d" % t],
                                 [K_("RR" if t == 0 else "IG", mo)], scale=0.5, bias=HGB[:, t, c:c + 1])
                    for j in range(2):
                        c = 2 * blk + j
                        ACTF(AA[:, j, 0:sz], RR[:, j, 0:sz], AF.Exp, [K_("RR", j), "HC"], [K_("AA", j)],
                             scale=HC[:, c:c + 1], bias=HC[:, c:c + 1])
                    rrk2 = [K_("RR", 0), K_("RR", 1)]
                    aak2 = [K_("AA", 0), K_("AA", 1)]
                    ACTF(RR[:, :, 0:sz], AA[:, :, 0:sz], AF.Square, aak2 + rrk2, rrk2)
                    ACTF(RR[:, :, 0:sz], RR[:, :, 0:sz], AF.Ln, rrk2, rrk2, scale=-1.0, bias=1.0)
                    ACTF(RR[:, :, 0:sz], RR[:, :, 0:sz], AF.Exp, rrk2, rrk2, scale=0.5, bias=LNHALF)

                def bd():
                    for j in range(2):
                        c = 2 * blk + j
                        rrk = [K_("RR", j)]
                        igk = [K_("IG", j)]
                        OP("dve", "scalar_tensor_tensor", IG[:, j, 0:sz], IG[:, j, 0:sz], 1.0, XC[:, j, 0:sz], op0=ALU.add, op1=ALU.mult,
                           reads=igk + [K_("XC", j)], writes=igk)
                        OP("dve", "tensor_tensor", IG[:, j, 0:sz], IG[:, j, 0:sz], RR[:, j, 0:sz], op=ALU.mult, reads=rrk + igk, writes=igk)
                        if hs:
                            a0 = AA[:, j, sl:sl + NSC].rearrange("p (s t) -> p s t", t=ST)[:, :, 0]
                            u0 = IG[:, j, sl:sl + NSC].rearrange("p (s t) -> p s t", t=ST)[:, :, 0]
                            OP("dve", "tensor_tensor", TMP16[:, j, :], a0, PST[:, c, R_H:R_H + NS], op=ALU.mult,
                               reads=[K_("AA", j)] + PSTK, writes=[("TMP16", j)])
                            OP("dve", "tensor_tensor", u0, u0, TMP16[:, j, :], op=ALU.add, reads=igk + [("TMP16", j)], writes=igk)
                            OP("dve", "memset", a0, 0.0, reads=[("TMP16", j)], writes=[K_("AA", j)])
                        OP("dve", "tensor_tensor_scan", HH[:, j, 0:sz], AA[:, j, 0:sz], IG[:, j, 0:sz], HST[:, c:c + 1],
                           op0=ALU.mult, op1=ALU.add, reads=[K_("AA", j), ("HST", c)] + igk, writes=[K_("RR", j)])
                        if tp > 0:
                            OP("dve", "tensor_copy", HST[:, c:c + 1], HH[:, j, tp - 1:tp], reads=[K_("RR", j)], writes=[("HST", c)])
                            if lastp_here:
                                OP("dve", "tensor_copy", OST[:, c, O_RGH_P:O_RGH_P + 1], HH[:, j, tp - 1:tp], reads=[K_("RR", j)], writes=[("OST", c)])
                        if hs:
                            OP("dve", "tensor_copy", OST[:, c, O_RGH_S:O_RGH_S + NS],
                               HH[:, j, sl:sl + NSC].rearrange("p (s t) -> p s t", t=ST)[:, :, ST - 1], reads=[K_("RR", j)], writes=[("OST", c)])
                        OP("dve", "tensor_tensor", G[:, c, o:o + sz], HH[:, j, 0:sz], GG[:, j, 0:sz], op=ALU.mult,
                           reads=[K_("RR", j), K_("GG", j)], writes=[("G", c, n)])

                return fb, ba, bd

            backs = []
            if "l0mix" not in skip and dbg >= 11:
                for bs_ in ([0, 1, 2, 3],):
                    sl_ = {}
                    for i, blk in enumerate(bs_):
                        sl_[blk] = next_weight(pending=i)
                    for n in allN:
                        if n == len(NT) - 1 and pend:
                            pend.pop()()
                        for blk in bs_:
                            fb_, ba_, bd_ = rg_unit(blk, n, *sl_[blk])
                            if backs:
                                backs[0][0]()
                            fb_()
                            if backs:
                                backs.pop(0)[1]()
                            backs.append((ba_, bd_))
            def drain_backs():
                while backs:
                    ba_, bd_ = backs.pop(0)
                    ba_()
                    bd_()

            if "l0mix" not in skip and dbg >= 15:
                if len(NT) > 1:
                    pend = mixer_out(NT, R_NORM + 2 * 1 + 0, (R_NORM + 2 * 2 + 0) if "ffn0" not in skip else None,
                                     hooks={(0, 3): drain_backs})
                else:
                    drain_backs()
                    pend = mixer_out(NT, R_NORM + 2 * 1 + 0, (R_NORM + 2 * 2 + 0) if "ffn0" not in skip else None)
            drain_backs()
            pend = ffn(NT, 0, (R_NORM + 1) if "l1mix" not in skip else None, pend)

            def sc_unit(c, n, wv, wkeys):
                o, sz, tp, hs, sl, lastp_here = tile_info(n)
                bsx = unit_i[0] % 2
                B_ = SCS[bsx]
                unit_i[0] += 1
                BG, CV, CS, CC, CG = (B_[k] for k in ("BG", "CV", "CS", "CC", "CG"))
                K_ = lambda nm: (nm, bsx)
                banks = {}
                for nm, mo in (("cg", 128), ("v", 256), ("bg", 0)):
                    b = next_bank()
                    banks[nm] = b
                    mm_group(PS[:, b, 0:sz], [(wv[:, k, mo:mo + 128], HN[:, k, o:o + sz]) for k in range(KC)],
                             lambda i: wkeys + [("HN", i, n)], b)
                ACTF(CG[:, 0:sz], PS[:, banks["cg"], 0:sz], AF.Copy, [("PS", banks["cg"])], [K_("CG")])
                OP("dve", "tensor_tensor", CV[:, 2:2 + sz], CG[:, 0:sz], PS[:, banks["v"], 0:sz], op=ALU.mult,
                   reads=[K_("CG"), ("PS", banks["v"])], writes=[K_("CV")])
                ACTF(BG[:, 0:sz], PS[:, banks["bg"], 0:sz], AF.Copy, [("PS", banks["bg"])], [K_("BG")])
                cvk = [K_("CV")]
                OP("dve", "tensor_copy", CV[:, 0:2], SCST[:, c, 0:2], reads=[("SCST", c)], writes=[K_("CVpre")])
                if tp > 0:
                    ACTF(CC[:, 0:tp], CV[:, 2:2 + tp], AF.Identity, cvk + PSTK, [K_("CC")], scale=pcol(c, R_SCW + 2))
                    for k in (1, 0):
                        OP("dve", "scalar_tensor_tensor", CC[:, 0:tp], CV[:, k:k + tp], pcol(c, R_SCW + k), CC[:, 0:tp],
                           op0=ALU.mult, op1=ALU.add, reads=cvk + [K_("CVpre"), K_("CC")] + PSTK, writes=[K_("CC")])
                    OP("dve", "tensor_copy", SCST[:, c, 0:2], CV[:, tp:tp + 2], reads=cvk + [K_("CVpre")], writes=[("SCST", c)])
                    if lastp_here:
                        OP("dve", "tensor_copy", OST[:, c, O_SC_P:O_SC_P + 2], CV[:, tp:tp + 2], reads=cvk + [K_("CVpre")], writes=[("OST", c)])
                if hs:
                    OP("dve", "tensor_copy", CS[:, :, 0:2], PST[:, c, R_SC:R_SC + 32].rearrange("p (s t) -> p s t", t=2),
                       reads=PSTK, writes=[K_("CS")])
                    OP("dve", "tensor_copy", CS[:, :, 2:6], CV[:, 2 + sl:2 + sl + NSC].rearrange("p (s t) -> p s t", t=ST),
                       reads=cvk + [K_("CS")], writes=[K_("CS")])
                    ccs = CC[:, sl:sl + NSC].rearrange("p (s t) -> p s t", t=ST)
                    OP("dve", "tensor_scalar", ccs, CS[:, :, 2:6], pcol(c, R_SCW + 2), None, op0=ALU.mult,
                       reads=[K_("CS")] + PSTK, writes=[K_("CC")])
                    for k in (1, 0):
                        OP("dve", "scalar_tensor_tensor", ccs, CS[:, :, k:k + ST], pcol(c, R_SCW + k), ccs,
                           op0=ALU.mult, op1=ALU.add, reads=[K_("CS"), K_("CC")] + PSTK, writes=[K_("CC")])
                    OP("dve", "tensor_copy", OST[:, c, O_SC_S:O_SC_S + 32].rearrange("p (s t) -> p s t", t=2), CS[:, :, 4:6],
                       reads=[K_("CS")], writes=[("OST", c)])
                OP("dve", "tensor_tensor", G[:, c, o:o + sz], BG[:, 0:sz], CC[:, 0:sz], op=ALU.mult,
                   reads=[K_("BG"), K_("CC")], writes=[("G", c, n)])

            if "l1mix" not in skip:
                for cs_ in ([0, 1, 2, 3], [4], [5], [6], [7]):
                    sl_ = {}
                    for i, c in enumerate(cs_):
                        sl_[c] = next_weight(pending=i)
                    for n in allN:
                        if n == len(NT) - 1 and pend:
                            pend.pop()()
                        for c in cs_:
                            sc_unit(c, n, *sl_[c])
            if "l1mix" not in skip:
                pend = mixer_out(NT, R_NORM + 2 * 1 + 1, (R_NORM + 2 * 2 + 1) if "ffn1" not in skip else None)
            hk, aa = group_tail_hooks(g)
            pend = ffn(NT, 1, None, pend, hk, aa)
            assert not pend

        if dbg >= 6:
            store_rows(lambda c: OST[:, c, 0:NOUTST], NOUTST, T, lambda c: [("OST", c)])
        S.final_wait("sp")

        @block.tensor
        def _(e):
            S.emit("pe", e, esem, dsems)

        @block.scalar
        def _(e):
            S.emit("act", e, esem, dsems)

        @block.vector
        def _(e):
            S.emit("dve", e, esem, dsems)

        @block.gpsimd
        def _(e):
            S.emit("pool", e, esem, dsems)

        @block.sync
        def _(e):
            S.emit("sp", e, esem, dsems)
    return nc


def make_core_inputs(c, x_prompt, x_sample, state_rglru_conv, state_rglru_h, state_sconv, meta_tokens,
                     norm_mix_pre, norm_mix_post, norm_ffn_pre, norm_ffn_post, rg_conv_w, rg_conv_b,
                     rg_gate_a_b, rg_gate_x_b, rg_lambda, sc_conv_w):
    f = lambda a: np.asarray(a, dtype=np.float32)
    s0, s1 = NS * c, NS * (c + 1)
    xin = np.concatenate([f(meta_tokens), f(x_prompt[c]), f(x_sample[s0:s1]).reshape(NSC, D)], axis=0)
    pst = np.concatenate([
        f(state_rglru_conv[0, s0:s1]).reshape(NS * 3, D),
        f(state_rglru_h[0, s0:s1]).reshape(NS, D),
        f(state_sconv[0, s0:s1]).reshape(NS * 2, D),
        f(norm_mix_pre), f(norm_mix_post), f(norm_ffn_pre), f(norm_ffn_post),
        f(rg_conv_w[0]), f(rg_conv_b), f(rg_gate_a_b).reshape(1, D), f(rg_gate_x_b).reshape(1, D),
        f(rg_lambda), f(sc_conv_w[0]),
    ], axis=0)
    assert pst.shape == (NPST, D), pst.shape
    return np.ascontiguousarray(xin), np.ascontiguousarray(pst)


_NC_CACHE = {}


def kernel(x_prompt, x_sample, state_rglru_conv, state_rglru_h, state_sconv, meta_tokens,
           norm_mix_pre, norm_mix_post, norm_ffn_pre, norm_ffn_post,
           rg_w_in, rg_conv_w, rg_conv_b, rg_gate_a_w, rg_gate_a_b, rg_gate_x_w, rg_gate_x_b,
           rg_lambda, rg_w_out, sc_w_in, sc_conv_w, sc_w_out,
           ffn_w_gate, ffn_w_up, ffn_w_down):
    f = lambda a: np.ascontiguousarray(np.asarray(a, dtype=np.float32))
    B, SEQ, _ = x_prompt.shape
    TP = NMETA + SEQ
    T = TP + NSC
    assert B == N_CORES and x_sample.shape[0] == NS * N_CORES and x_sample.shape[1] == ST
    if TP not in _NC_CACHE:
        _NC_CACHE[TP] = build(TP)
    nc = _NC_CACHE[TP]
    shared = {
        "ident": np.eye(128, dtype=np.float32),
        "rg_w_in": f(rg_w_in[0]), "ga_w": f(rg_gate_a_w[0]), "gx_w": f(rg_gate_x_w[0]),
        "rg_w_out": f(rg_w_out[0]), "sc_w_in": f(sc_w_in[0]), "sc_w_out": f(sc_w_out[0]),
        "w_gate": f(ffn_w_gate), "w_up": f(ffn_w_up), "w_down": f(ffn_w_down),
    }
    in_maps = []
    for c in range(N_CORES):
        xin, pst = make_core_inputs(c, x_prompt, x_sample, state_rglru_conv, state_rglru_h, state_sconv, meta_tokens,
                                    norm_mix_pre, norm_mix_post, norm_ffn_pre, norm_ffn_post, rg_conv_w, rg_conv_b,
                                    rg_gate_a_b, rg_gate_x_b, rg_lambda, sc_conv_w)
        m = dict(shared)
        m["xin"] = xin
        m["pst"] = pst
        in_maps.append(m)
    res = run_bass_kernel_spmd(nc, in_maps, core_ids=list(range(N_CORES)))
    outs = [np.asarray(r["yout"], dtype=np.float32) for r in res.results]
    y_prompt = np.stack([o[NMETA:TP] for o in outs], axis=0)
    y_sample = np.concatenate([o[TP:T].reshape(NS, ST, D) for o in outs], axis=0)
    st = [o[T:T + NOUTST] for o in outs]
    rg_conv_p = np.stack([s[O_RGC_P:O_RGC_P + 3] for s in st], axis=0)[None]
    rg_h_p = np.stack([s[O_RGH_P] for s in st], axis=0)[None]
    sc_p = np.stack([s[O_SC_P:O_SC_P + 2] for s in st], axis=0)[None]
    rg_conv_s = np.concatenate([s[O_RGC_S:O_RGC_S + 48].reshape(NS, 3, D) for s in st], axis=0)[None]
    rg_h_s = np.concatenate([s[O_RGH_S:O_RGH_S + NS] for s in st], axis=0)[None]
    sc_s = np.concatenate([s[O_SC_S:O_SC_S + 32].reshape(NS, 2, D) for s in st], axis=0)[None]
    c = np.ascontiguousarray
    return (c(y_prompt), c(y_sample), c(rg_conv_p), c(rg_h_p), c(sc_p), c(rg_conv_s), c(rg_h_s), c(sc_s))
```

```python
import numpy as np
import concourse.bass as bass
import concourse.mybir as mybir
from concourse.bass_utils import run_bass_kernel_spmd

F32 = mybir.dt.float32
BF16 = mybir.dt.bfloat16
AF = mybir.ActivationFunctionType
ALU = mybir.AluOpType

D = 1024
KC = 8
DFF = 2816
FC = 22
NMETA = 16
EPS = 1e-6
NS = 16
ST = 4
NSC = NS * ST
N_CORES = 8

R_CONV, R_H, R_SC, R_NORM, R_RGCW, R_RGCB, R_GAB, R_GXB, R_LAM, R_SCW = 0, 48, 64, 96, 104, 108, 109, 110, 111, 112
NPST = 115
O_RGC_P, O_RGH_P, O_SC_P, O_RGC_S, O_RGH_S, O_SC_S = 0, 3, 4, 6, 54, 70
NOUTST = 102
LNHALF = -0.6931471805599453

ENGS = ("pe", "act", "dve", "pool", "sp")
NSLOT = 6
SLOT_ELEMS = 4096
N_DSEM = 16


class Sched:
    def __init__(self):
        self.q = {e: [] for e in ENGS}
        self.cnt = {e: 0 for e in ENGS}
        self.seen = {e: {} for e in ENGS}
        self.res = {}
        self.dval = [0] * N_DSEM

    def _deps(self, eng, reads, writes):
        deps = {}

        def need(k, v):
            if k == "pe" and eng == "pe":
                return
            if deps.get(k, 0) < v:
                deps[k] = v

        for r in reads:
            e = self.res.get(r)
            if e and e[0]:
                need(*e[0])
        for w in writes:
            e = self.res.get(w)
            if e:
                if e[0]:
                    need(*e[0])
                for k, v in e[1].items():
                    need(k, v)
        out = []
        for k, v in deps.items():
            if self.seen[eng].get(k, 0) < v:
                self.seen[eng][k] = v
                out.append((k, v))
        return out

    def _record(self, tok, reads, writes):
        for r in reads:
            e = self.res.setdefault(r, [None, {}])
            if e[1].get(tok[0], 0) < tok[1]:
                e[1][tok[0]] = tok[1]
        for w in writes:
            self.res[w] = [tok, {}]

    def op(self, eng, fn, reads=(), writes=(), inc=True):
        waits = self._deps(eng, reads, writes)
        tok = (eng, self.cnt[eng] + 1)
        if inc:
            self.cnt[eng] += 1
        self._record(tok, reads, writes)
        self.q[eng].append((waits, fn, eng if inc else None, False))
        return tok

    def dma(self, eng, fn, dsem, reads=(), writes=(), final=None):
        waits = self._deps(eng, reads, writes)
        self.dval[dsem] += 16
        tok = (("d", dsem), final if final is not None else self.dval[dsem])
        self._record(tok, reads, writes)
        self.q[eng].append((waits, fn, ("d", dsem), True))
        return tok

    def barrier(self, engs=("pe", "act", "dve")):
        for e in engs:
            waits = []
            for o in engs:
                if (o != e or e != "pe") and self.seen[e].get(o, 0) < self.cnt[o]:
                    self.seen[e][o] = self.cnt[o]
                    waits.append((o, self.cnt[o]))
            if waits:
                self.q[e].append((waits, None, None, False))

    def final_wait(self, eng):
        waits = []
        for e in ("pe", "act", "dve", "pool"):
            if e != eng and self.cnt[e] > 0:
                waits.append((e, self.cnt[e]))
        for i, v in enumerate(self.dval):
            if v > 0:
                waits.append((("d", i), v))
        self.q[eng].append((waits, None, None, False))

    def emit(self, name, eng, esem, dsems):
        def sem_of(k):
            return dsems[k[1]] if isinstance(k, tuple) else esem[k]

        for waits, fn, upd, is_dma in self.q[name]:
            sems = [(sem_of(k), v) for k, v in waits]
            if fn is None:
                for s, v in sems:
                    eng.wait_ge(s, v)
                continue
            attach = None
            if sems and not is_dma:
                attach = sems[0]
                sems = sems[1:]
            for s, v in sems:
                eng.wait_ge(s, v)
            ins = fn(eng)
            if attach is not None:
                ins._wait_ge(attach[0], attach[1])
            if upd is not None:
                if is_dma:
                    ins.then_inc(dsems[upd[1]], 16)
                else:
                    ins.then_inc(esem[upd], 1)


def split_tiles(n, maxw=512):
    k = (n + maxw - 1) // maxw
    if n > 256 and k < 2:
        k = 2
    base, rem = divmod(n, k)
    out, o = [], 0
    for i in range(k):
        s = base + (1 if i < rem else 0)
        out.append((o, s))
        o += s
    return out


def make_groups(T, ng):
    base, rem = divmod(T, ng)
    out, o = [], 0
    for i in range(ng):
        s = base + (1 if i < rem else 0)
        out.append((o, o + s))
        o += s
    return out


def build(TP, NG=3, skip=(), dbg=99):
    T = TP + NSC
    groups = make_groups(T, NG)
    TG = max(c1 - c0 for c0, c1 in groups)
    assert groups[-1][0] < TP - 3 and all(c1 <= TP for c0, c1 in groups[:-1]) and all(min(c1, TP) - c0 >= 3 for c0, c1 in groups)
    NTW = max(s for c0, c1 in groups for o, s in split_tiles(c1 - c0))

    nc = bass.Bass("TRN2", target_bir_lowering=False)
    dr = lambda n, s: nc.dram_tensor(n, s, F32, kind="ExternalInput").ap()
    xin = dr("xin", [T, D])
    pst = dr("pst", [NPST, D])
    ident_d = dr("ident", [128, 128])
    rg_w_in = dr("rg_w_in", [D, 2 * D])
    ga_w = dr("ga_w", [4, 256, 256])
    gx_w = dr("gx_w", [4, 256, 256])
    rg_w_out = dr("rg_w_out", [D, D])
    sc_w_in = dr("sc_w_in", [D, 3 * D])
    sc_w_out = dr("sc_w_out", [D, D])
    w_gate = dr("w_gate", [2, D, DFF])
    w_up = dr("w_up", [2, D, DFF])
    w_down = dr("w_down", [2, DFF, D])
    yout = nc.dram_tensor("yout", [T + NOUTST, D], F32, kind="ExternalOutput").ap()

    from contextlib import ExitStack
    S = Sched()
    with ExitStack() as es:
        sb = lambda n, s, dt=F32: es.enter_context(nc.sbuf_tensor(n, s, dt))
        X = sb("X", [128, KC, TG])
        HN = sb("HN", [128, KC, TG], BF16)
        G = sb("G", [128, KC, TG], BF16)
        SQ = sb("SQ", [128, KC, TG], BF16)
        Y = sb("Y", [128, KC, TG])
        RSTD = sb("RSTD", [128, TG])
        ARENA_B = max(FC * TG * 2, 2 * ((2 * (NTW + 3) + 2 * NS * 7 + 10 * NTW) * 4 + (2 * NTW + 2) * 2) + 64)
        ARENA_B = (ARENA_B + 63) // 64 * 64
        AR = sb("AR", [128, ARENA_B // 4])
        WR = sb("WR", [128, NSLOT, SLOT_ELEMS], BF16)
        GW = sb("GW", [128, 2, 4, 2, 256], BF16)
        INB = sb("INB", [128, 2, D])
        OUTB = sb("OUTB", [128, 2, D])
        PST = sb("PST", [128, KC, 128])
        OST = sb("OST", [128, KC, 128])
        IDENT = sb("IDENT", [128, 128])
        ONES = sb("ONES", [128, 128], BF16)
        CONVST = sb("CONVST", [128, KC, 4])
        SCST = sb("SCST", [128, KC, 2])
        HST = sb("HST", [128, KC])
        CNEG = sb("CNEG", [128, KC])
        C2NEG = sb("C2NEG", [128, KC])
        TMPV = sb("TMPV", [128, KC])
        HC = sb("HC", [128, KC])
        HGB = sb("HGB", [128, 2, KC])
        TMP16 = sb("TMP16", [128, 2, NS])
        SIL = sb("SIL", [128, 2, NTW])
        PS = es.enter_context(nc.psum_tensor("PS", [128, 8, 512], F32))
        esem = {e: es.enter_context(nc.semaphore("s_" + e)) for e in ENGS}
        dsems = [es.enter_context(nc.semaphore("d%d" % i)) for i in range(N_DSEM)]
        block = es.enter_context(nc.Block())

        arena_f32 = AR[:, :]
        arena_bf = AR[:, :].bitcast(BF16)
        ACTB = arena_bf[:, 0:FC * TG].rearrange("p (j t) -> p j t", t=TG)
        off = [0]

        def carve(nelem, dt=F32, shape=None):
            o = off[0]
            if dt == F32:
                v = arena_f32[:, o:o + nelem]
                off[0] += nelem
            else:
                v = arena_bf[:, 2 * o:2 * o + nelem]
                off[0] += (nelem + 1) // 2
            return v

        W_ = NTW
        off[0] = 0
        RGS = []
        for i in range(2):
            d = {}
            d["XR"] = carve(2 * (W_ + 3)).rearrange("p (j t) -> p j t", t=W_ + 3)
            d["XRS"] = carve(2 * NS * 7).rearrange("p (j s t) -> p j s t", s=NS, t=7)
            for nm in ("GG", "XC", "RR", "IG", "AA"):
                d[nm] = carve(2 * W_).rearrange("p (j t) -> p j t", t=W_)
            d["HH"] = d["RR"]
            d["XCB"] = carve(2 * W_ + 2, BF16)[:, 0:2 * W_].rearrange("p (j t) -> p j t", t=W_)
            RGS.append(d)
        rg_words = off[0]
        off[0] = 0
        SCS = []
        for i in range(2):
            d = {}
            d["BG"] = carve(W_)
            d["CV"] = carve(W_ + 2)
            d["CS"] = carve(NS * 6).rearrange("p (s t) -> p s t", t=6)
            d["CC"] = carve(W_)
            d["CG"] = carve(W_)
            SCS.append(d)
        assert max(rg_words, off[0]) * 4 <= ARENA_B, (rg_words, off[0], ARENA_B)

        def OP(eng, method, *args, reads=(), writes=(), inc=True, **kw):
            return S.op(eng, lambda e: getattr(e, method)(*args, **kw), reads, writes, inc)

        def DMA(eng, dsem, out, in_, reads=(), writes=(), final=None):
            return S.dma(eng, lambda e: e.dma_start(out=out, in_=in_), dsem, reads, writes, final)

        def ACTF(out, in_, func, reads, writes, **kw):
            return OP("act", "activation", out=out, in_=in_, func=func, reads=reads, writes=writes, **kw)

        bank_i = [0]

        def next_bank():
            b = bank_i[0] % 8
            bank_i[0] += 1
            return b

        def mm_group(out_ap, pairs, reads_of, bank):
            n = len(pairs)
            for i, (lhsT, rhs) in enumerate(pairs):
                OP("pe", "matmul", out_ap, lhsT, rhs, start=(i == 0), stop=(i == n - 1),
                   reads=reads_of(i), writes=[("PS", bank)], inc=(i == n - 1))

        def plan_weights():
            plan = []
            for g in range(NG):
                nnt = len(split_tiles(groups[g][1] - groups[g][0]))
                for b in range(4 if ("l0mix" not in skip and dbg >= 11) else 0):
                    plan.append((8, 512, [(rg_w_in, 256 * b, 256, 0), (rg_w_in, D + 256 * b, 256, 256)]))
                for q in range(2 if ("l0mix" not in skip and dbg >= 15) else 0):
                    plan.append((8, 512, [(rg_w_out, 512 * q, 512, 0)]))
                for L in range(2):
                    if L == 1 and "l1mix" not in skip:
                        for c in range(KC):
                            plan.append((8, 384, [(sc_w_in, 128 * c, 128, 0), (sc_w_in, D + 128 * c, 128, 128),
                                                  (sc_w_in, 2 * D + 128 * c, 128, 256)]))
                        for q in range(2):
                            plan.append((8, 512, [(sc_w_out, 512 * q, 512, 0)]))
                    if ("ffn%d" % L) in skip:
                        continue
                    for q in range(6):
                        w = min(512, DFF - 512 * q)
                        plan.append((8, w, [(w_gate[L], 512 * q, w, 0)]))
                        plan.append((8, w, [(w_up[L], 512 * q, w, 0)]))
                    for n_ in range(nnt):
                        for m in range(KC):
                            plan.append((FC, 128, [(w_down[L], 128 * m, 128, 0)]))
            return plan

        wstate = {"next_use": 0, "issued": 0, "plan": plan_weights()}

        def wslot_view(u):
            K, width, parts = wstate["plan"][u]
            s = u % NSLOT
            return WR[:, s, 0:K * width].rearrange("p (k n) -> p k n", n=width)

        def issue_weight(u):
            K, width, parts = wstate["plan"][u]
            s = u % NSLOT
            view = wslot_view(u)
            fin = S.dval[s] + 16 * len(parts)
            for pi, (w, c0, wd, doff) in enumerate(parts):
                src = w.rearrange("(k p) n -> p k n", p=128)[:, :, c0:c0 + wd]
                DMA("pool", s, view[:, :, doff:doff + wd], src, writes=[("W", s, pi)], final=fin)

        def next_weight(pending=0):
            u = wstate["next_use"]
            wstate["next_use"] += 1
            while wstate["issued"] < min(len(wstate["plan"]), u + NSLOT - pending):
                issue_weight(wstate["issued"])
                wstate["issued"] += 1
            s = u % NSLOT
            return wslot_view(u), [("W", s, 0), ("W", s, 1), ("W", s, 2)]

        DMA("sp", 12, IDENT[:, :], ident_d[:, :], writes=["IDENT"])
        DMA("sp", 13, INB[0:NPST, 0, :], pst[:, :], writes=[("INB", 0)])
        OP("dve", "memset", ONES[:, :], 1.0, writes=["ONES"])
        OP("dve", "memset", CONVST[:, :, :], 0.0, writes=[("CONVST", c) for c in range(KC)])
        OP("dve", "memset", SCST[:, :, :], 0.0, writes=[("SCST", c) for c in range(KC)])
        OP("dve", "memset", HST[:, :], 0.0, writes=[("HST", c) for c in range(KC)])
        OP("dve", "memset", OST[:, :, :], 0.0, writes=[("OST", c) for c in range(KC)])
        for t, gwd in enumerate((ga_w, gx_w) if dbg >= 2 else ()):
            for b in range(4):
                DMA("pool", 6 + t, GW[:, t, b, :, :], gwd[b].rearrange("(ki p) n -> p ki n", p=128), writes=[("GW", t, b)], final=64)
        GWK = [[("GW", t, b) for b in range(4)] for t in range(2)]

        def transpose_in(slot, rows, dst_of_half, keys_of_half, alt=0):
            for half in range(2):
                b = next_bank()
                for q in range(4):
                    c = half * 4 + q
                    OP("pe", "transpose", PS[:, b, q * 128:q * 128 + rows], INB[0:rows, slot, c * 128:(c + 1) * 128],
                       IDENT[0:rows, 0:rows], reads=[("INB", slot), "IDENT"], writes=[("PS", b)], inc=(q == 3))
                src = PS[:, b, :].rearrange("p (q r) -> p q r", r=128)[:, :, 0:rows]
                if (half + alt) % 2 == 0:
                    OP("dve", "tensor_copy", dst_of_half(half), src, reads=[("PS", b)], writes=keys_of_half(half))
                else:
                    ACTF(dst_of_half(half), src, AF.Copy, [("PS", b)], keys_of_half(half))

        if dbg >= 3:
            transpose_in(0, NPST, lambda h: PST[:, h * 4:(h + 1) * 4, 0:NPST], lambda h: [("PST", h)])
        PSTK = [("PST", 0), ("PST", 1)]
        if dbg >= 4:
            ACTF(TMPV[:, :], PST[:, :, R_LAM], AF.Exp, PSTK, ["TMPV"], scale=-1.0)
            ACTF(TMPV[:, :], TMPV[:, :], AF.Ln, ["TMPV"], ["TMPV"], bias=1.0)
            OP("dve", "tensor_scalar", CNEG[:, :], TMPV[:, :], -8.0, None, op0=ALU.mult, reads=["TMPV"], writes=["CNEG"])
            OP("dve", "tensor_scalar", C2NEG[:, :], TMPV[:, :], -16.0, None, op0=ALU.mult, reads=["TMPV"], writes=["C2NEG"])
            OP("dve", "tensor_scalar", HC[:, :], TMPV[:, :], -4.0, None, op0=ALU.mult, reads=["TMPV"], writes=["HC"])
            OP("dve", "tensor_scalar", HGB[:, 0, :], PST[:, :, R_GAB], 0.5, None, op0=ALU.mult, reads=PSTK, writes=["HGB0"])
            OP("dve", "tensor_scalar", HGB[:, 1, :], PST[:, :, R_GXB], 0.5, None, op0=ALU.mult, reads=PSTK, writes=["HGB1"])

        def pcol(c, r):
            return PST[:, c, r:r + 1]

        in_slot = [1]
        out_slot = [0]

        def norm_rstd(n, o, sz):
            b = next_bank()
            mm_group(PS[:, b, 0:sz], [(ONES[:, :], SQ[:, c, o:o + sz]) for c in range(KC)],
                     lambda i: [("SQ", i, n), "ONES"], b)
            ACTF(RSTD[:, o:o + sz], PS[:, b, 0:sz], AF.Ln, [("PS", b)], [("RSTD", n)], scale=1.0 / D, bias=EPS)
            ACTF(RSTD[:, o:o + sz], RSTD[:, o:o + sz], AF.Exp, [("RSTD", n)], [("RSTD", n)], scale=-0.5)

        def prenorm_n(n, o, sz, nrow):
            ACTF(SQ[:, :, o:o + sz], X[:, :, o:o + sz], AF.Square,
                 [("X", c, n) for c in range(KC)], [("SQ", c, n) for c in range(KC)])
            norm_rstd(n, o, sz)
            for c in range(KC):
                OP("dve", "scalar_tensor_tensor", HN[:, c, o:o + sz], X[:, c, o:o + sz], pcol(c, nrow), RSTD[:, o:o + sz],
                   op0=ALU.mult, op1=ALU.mult, reads=[("X", c, n), ("RSTD", n)] + PSTK, writes=[("HN", c, n)])

        def prenorm(NT, nrow):
            for n, (o, sz) in enumerate(NT):
                prenorm_n(n, o, sz, nrow)

        def postnorm_n(n, o, sz, nrow):
            norm_rstd(n, o, sz)
            for c in range(KC):
                OP("dve", "scalar_tensor_tensor", Y[:, c, o:o + sz], Y[:, c, o:o + sz], pcol(c, nrow), RSTD[:, o:o + sz],
                   op0=ALU.mult, op1=ALU.mult, reads=[("Y", c, n), ("RSTD", n)] + PSTK, writes=[("Y", c, n)])
                OP("dve", "tensor_tensor", X[:, c, o:o + sz], X[:, c, o:o + sz], Y[:, c, o:o + sz], op=ALU.add,
                   reads=[("X", c, n), ("Y", c, n)], writes=[("X", c, n)])

        def proj_out(NT, src, srckey, nk, wview, post_row, next_pre_row, mh, hooks=None, after_all=None):
            for n, (o, sz) in enumerate(NT):
                for m in range(KC):
                    wv, wkeys, mo = wview(m, n)
                    b = next_bank()
                    mm_group(PS[:, b, 0:sz], [(wv[:, k, mo:mo + 128], src[:, k, o:o + sz]) for k in range(nk)],
                             lambda i: wkeys + [(srckey, i, n)], b)
                    ACTF(Y[:, m, o:o + sz], PS[:, b, 0:sz], AF.Copy, [("PS", b)], [("Y", m, n)])
                    ACTF(SQ[:, m, o:o + sz], PS[:, b, 0:sz], AF.Square, [("PS", b)], [("SQ", m, n)])
                    if next_pre_row is not None and n > 0 and m == mh:
                        po, psz = NT[n - 1]
                        prenorm_n(n - 1, po, psz, next_pre_row)
                    if hooks and (n, m) in hooks:
                        hooks[(n, m)]()
                postnorm_n(n, o, sz, post_row)
            if after_all:
                after_all()
            if next_pre_row is not None:
                po, psz = NT[-1]
                return [lambda: prenorm_n(len(NT) - 1, po, psz, next_pre_row)]
            return []

        def mixer_out(NT, post_row, next_pre_row, hooks=None):
            v0, k0 = next_weight()
            v1, k1 = next_weight(pending=1)
            views = [(v0, k0), (v1, k1)]
            return proj_out(NT, G, "G", KC, lambda m, n: (views[m // 4][0], views[m // 4][1], (m % 4) * 128), post_row, next_pre_row, 6, hooks)

        def ffn(NT, L, next_pre_row, pend, hooks=None, after_all=None):
            if ("ffn%d" % L) in skip:
                if after_all:
                    after_all()
                return pend
            for qs in ([0, 1], [2], [3], [4], [5]):
                slots = {}
                for i, q in enumerate(qs):
                    slots[q] = (next_weight(pending=2 * i), next_weight(pending=2 * i + 1))
                for n, (o, sz) in enumerate(NT):
                    if n == len(NT) - 1 and pend:
                        pend.pop()()
                    for q in qs:
                        w = min(512, DFF - 512 * q)
                        (wg, kg), (wu, ku) = slots[q]
                        for jj in range(w // 128):
                            j = 4 * q + jj
                            bg_ = next_bank()
                            bu_ = next_bank()
                            for (bb, wv, wk) in ((bg_, wg, kg), (bu_, wu, ku)):
                                mm_group(PS[:, bb, 0:sz], [(wv[:, k, jj * 128:(jj + 1) * 128], HN[:, k, o:o + sz]) for k in range(KC)],
                                         lambda i, wk=wk: wk + [("HN", i, n)], bb)
                            si = (j * len(NT) + n) % 2
                            ACTF(SIL[:, si, 0:sz], PS[:, bg_, 0:sz], AF.Silu, [("PS", bg_)], [("SIL", si)])
                            OP("dve", "tensor_tensor", ACTB[:, j, o:o + sz], SIL[:, si, 0:sz], PS[:, bu_, 0:sz], op=ALU.mult,
                               reads=[("SIL", si), ("PS", bu_)], writes=[("ACTB", j, n)])

            def wview(m, n):
                v, k = next_weight()
                return v, k, 0

            return proj_out(NT, ACTB, "ACTB", FC, wview, R_NORM + 2 * 3 + L, next_pre_row, 3, hooks, after_all)

        def store_rows(src_of_c, rows, drow0, keys_of_c):
            slot = out_slot[0] % 2
            out_slot[0] += 1
            for half in range(2):
                b = next_bank()
                for q in range(4):
                    c = half * 4 + q
                    OP("pe", "transpose", PS[0:rows, b, q * 128:(q + 1) * 128], src_of_c(c), IDENT[:, :],
                       reads=["IDENT"] + keys_of_c(c), writes=[("PS", b)], inc=(q == 3))
                if half == 0:
                    OP("dve", "tensor_copy", OUTB[0:rows, slot, 0:512], PS[0:rows, b, :], reads=[("PS", b)], writes=[("OUTB", slot, 0)])
                else:
                    ACTF(OUTB[0:rows, slot, 512:1024], PS[0:rows, b, :], AF.Copy, [("PS", b)], [("OUTB", slot, 1)])
            DMA("sp", 8 + slot, yout[drow0:drow0 + rows, :], OUTB[0:rows, slot, :], reads=[("OUTB", slot, 0), ("OUTB", slot, 1)])

        NTs = [split_tiles(c1_ - c0_) for (c0_, c1_) in groups]

        def emit_load(g, n):
            c0g = groups[g][0]
            o, sz = NTs[g][n]
            for lo in range(o, o + sz, 128):
                rows = min(128, o + sz - lo)
                slot = in_slot[0] % 2
                in_slot[0] += 1
                DMA("sp", 10 + slot, INB[0:rows, slot, :], xin[c0g + lo:c0g + lo + rows, :], writes=[("INB", slot)])
                transpose_in(slot, rows, lambda h: X[:, h * 4:(h + 1) * 4, lo:lo + rows],
                             lambda h: [("X", c, n) for c in range(4 * h, 4 * h + 4)], alt=(lo // 128))

        def emit_store(g, n):
            c0g = groups[g][0]
            o, sz = NTs[g][n]
            for lo in range(o, o + sz, 128):
                rows = min(128, o + sz - lo)
                store_rows(lambda c: X[:, c, lo:lo + rows], rows, c0g + lo, lambda c: [("X", c, n)])

        def emit_l0_pre(g, n):
            if "l0mix" not in skip:
                prenorm_n(n, NTs[g][n][0], NTs[g][n][1], R_NORM + 0)

        def group_tail_hooks(g):
            L_ = len(NTs[g])
            nxt = g + 1 if g + 1 < len(groups) else None
            Ln = len(NTs[nxt]) if nxt is not None else 0
            hooks = {}
            for n in range(1, L_):
                hooks[(n, 3)] = (lambda n=n: emit_store(g, n - 1))
                if nxt is not None and n - 1 < Ln:
                    hooks[(n, 5)] = (lambda n=n: emit_load(nxt, n - 1))
                    if n - 1 < Ln - 1:
                        hooks[(n, 6)] = (lambda n=n: emit_l0_pre(nxt, n - 1))

            def after_all():
                emit_store(g, L_ - 1)
                if nxt is not None:
                    for n2 in range(L_ - 1, Ln):
                        emit_load(nxt, n2)
                        if n2 < Ln - 1:
                            emit_l0_pre(nxt, n2)
            return hooks, after_all

        for g, (c0, c1) in enumerate(groups if dbg >= 5 else ()):
            Tg = c1 - c0
            Tp = min(c1, TP) - c0
            has_s = c1 > TP
            so = Tp
            last_p = (min(c1, TP) == TP)
            NT = split_tiles(Tg)
            allN = list(range(len(NT)))

            if g == 0:
                for n_ in allN:
                    emit_load(0, n_)
                    if n_ < len(NT) - 1:
                        emit_l0_pre(0, n_)

            def tile_info(n):
                o, sz = NT[n]
                tp = max(0, min(Tp - o, sz))
                assert tp == 0 or tp >= 3
                hs = has_s and (o <= so < o + sz)
                if has_s:
                    assert (o <= so and so + NSC <= o + sz) or (so + NSC <= o or so >= o + sz)
                return o, sz, tp, hs, (so - o), (last_p and tp > 0 and o + tp == Tp)

            pend = []
            if "l0mix" not in skip:
                pend = [lambda: prenorm_n(len(NT) - 1, NT[-1][0], NT[-1][1], R_NORM + 0)]
            unit_i = [0]

            def rg_unit(blk, n, wv, wkeys):
                o, sz, tp, hs, sl, lastp_here = tile_info(n)
                B_ = RGS[unit_i[0] % 2]
                bsx = unit_i[0] % 2
                unit_i[0] += 1
                XR, XRS, GG, XC, RR, IG, AA, HH, XCB = (B_[k] for k in ("XR", "XRS", "GG", "XC", "RR", "IG", "AA", "HH", "XCB"))
                K_ = lambda nm, j: (nm, bsx, j)
                for mi in (2, 3, 0, 1):
                    j = mi % 2
                    b = next_bank()
                    mm_group(PS[:, b, 0:sz], [(wv[:, k, mi * 128:(mi + 1) * 128], HN[:, k, o:o + sz]) for k in range(KC)],
                             lambda i: wkeys + [("HN", i, n)], b)
                    if mi >= 2:
                        OP("dve", "tensor_copy", XR[:, j, 3:3 + sz], PS[:, b, 0:sz], reads=[("PS", b)], writes=[K_("XR", j)])
                    else:
                        ACTF(GG[:, j, 0:sz], PS[:, b, 0:sz], AF.Gelu_apprx_tanh, [("PS", b)], [K_("GG", j)])
                for j in range(2):
                    c = 2 * blk + j
                    xrk = [K_("XR", j)]
                    OP("dve", "tensor_copy", XR[:, j, 0:3], CONVST[:, c, 0:3], reads=[("CONVST", c)], writes=[K_("XRpre", j)])
                    if tp > 0:
                        OP("dve", "tensor_scalar", XC[:, j, 0:tp], XR[:, j, 3:3 + tp], pcol(c, R_RGCW + 3), pcol(c, R_RGCB),
                           op0=ALU.mult, op1=ALU.add, reads=xrk + PSTK, writes=[K_("XC", j)])
                        for k in range(3):
                            OP("dve", "scalar_tensor_tensor", XC[:, j, 0:tp], XR[:, j, k:k + tp], pcol(c, R_RGCW + k), XC[:, j, 0:tp],
                               op0=ALU.mult, op1=ALU.add, reads=xrk + [K_("XRpre", j), K_("XC", j)] + PSTK, writes=[K_("XC", j)])
                        OP("dve", "tensor_copy", CONVST[:, c, 0:3], XR[:, j, tp:tp + 3], reads=xrk + [K_("XRpre", j)], writes=[("CONVST", c)])
                        if lastp_here:
                            OP("dve", "tensor_copy", OST[:, c, O_RGC_P:O_RGC_P + 3], XR[:, j, tp:tp + 3],
                               reads=xrk + [K_("XRpre", j)], writes=[("OST", c)])
                    if hs:
                        OP("dve", "tensor_copy", XRS[:, j, :, 0:3], PST[:, c, R_CONV:R_CONV + 48].rearrange("p (s t) -> p s t", t=3),
                           reads=PSTK, writes=[K_("XRS", j)])
                        OP("dve", "tensor_copy", XRS[:, j, :, 3:7], XR[:, j, 3 + sl:3 + sl + NSC].rearrange("p (s t) -> p s t", t=ST),
                           reads=xrk + [K_("XRS", j)], writes=[K_("XRS", j)])
                        xcs = XC[:, j, sl:sl + NSC].rearrange("p (s t) -> p s t", t=ST)
                        OP("dve", "tensor_scalar", xcs, XRS[:, j, :, 3:7], pcol(c, R_RGCW + 3), pcol(c, R_RGCB),
                           op0=ALU.mult, op1=ALU.add, reads=[K_("XRS", j)] + PSTK, writes=[K_("XC", j)])
                        for k in range(3):
                            OP("dve", "scalar_tensor_tensor", xcs, XRS[:, j, :, k:k + ST], pcol(c, R_RGCW + k), xcs,
                               op0=ALU.mult, op1=ALU.add, reads=[K_("XRS", j), K_("XC", j)] + PSTK, writes=[K_("XC", j)])
                        OP("dve", "tensor_copy", OST[:, c, O_RGC_S:O_RGC_S + 48].rearrange("p (s t) -> p s t", t=3), XRS[:, j, :, 4:7],
                           reads=[K_("XRS", j)], writes=[("OST", c)])

                def fb():
                    ACTF(XCB[:, :, 0:sz], XC[:, :, 0:sz], AF.Copy, [K_("XC", 0), K_("XC", 1)], [K_("XCB", 0), K_("XCB", 1)])

                def ba():
                    for t in range(2):
                        for mo in range(2):
                            c = 2 * blk + mo
                            b = next_bank()
                            mm_group(PS[:, b, 0:sz], [(GW[:, t, blk, ki, mo * 128:(mo + 1) * 128], XCB[:, ki, 0:sz]) for ki in range(2)],
                                     lambda i: GWK[t] + [K_("XCB", i)], b)
                            dstb = RR if t == 0 else IG
                            ACTF(dstb[:, mo, 0:sz], PS[:, b, 0:sz], AF.Tanh, [("PS", b), "HGB%d" % t],
                                 [K_("RR" if t == 0 else "IG", mo)], scale=0.5, bias=HGB[:, t, c:c + 1])
                    for j in range(2):
                        c = 2 * blk + j
                        ACTF(AA[:, j, 0:sz], RR[:, j, 0:sz], AF.Exp, [K_("RR", j), "HC"], [K_("AA", j)],
                             scale=HC[:, c:c + 1], bias=HC[:, c:c + 1])
                    rrk2 = [K_("RR", 0), K_("RR", 1)]
                    aak2 = [K_("AA", 0), K_("AA", 1)]
                    ACTF(RR[:, :, 0:sz], AA[:, :, 0:sz], AF.Square, aak2 + rrk2, rrk2)
                    ACTF(RR[:, :, 0:sz], RR[:, :, 0:sz], AF.Ln, rrk2, rrk2, scale=-1.0, bias=1.0)
                    ACTF(RR[:, :, 0:sz], RR[:, :, 0:sz], AF.Exp, rrk2, rrk2, scale=0.5, bias=LNHALF)

                def bd():
                    for j in range(2):
                        c = 2 * blk + j
                        rrk = [K_("RR", j)]
                        igk = [K_("IG", j)]
                        OP("dve", "scalar_tensor_tensor", IG[:, j, 0:sz], IG[:, j, 0:sz], 1.0, XC[:, j, 0:sz], op0=ALU.add, op1=ALU.mult,
                           reads=igk + [K_("XC", j)], writes=igk)
                        OP("dve", "tensor_tensor", IG[:, j, 0:sz], IG[:, j, 0:sz], RR[:, j, 0:sz], op=ALU.mult, reads=rrk + igk, writes=igk)
                        if hs:
                            a0 = AA[:, j, sl:sl + NSC].rearrange("p (s t) -> p s t", t=ST)[:, :, 0]
                            u0 = IG[:, j, sl:sl + NSC].rearrange("p (s t) -> p s t", t=ST)[:, :, 0]
                            OP("dve", "tensor_tensor", TMP16[:, j, :], a0, PST[:, c, R_H:R_H + NS], op=ALU.mult,
                               reads=[K_("AA", j)] + PSTK, writes=[("TMP16", j)])
                            OP("dve", "tensor_tensor", u0, u0, TMP16[:, j, :], op=ALU.add, reads=igk + [("TMP16", j)], writes=igk)
                            OP("dve", "memset", a0, 0.0, reads=[("TMP16", j)], writes=[K_("AA", j)])
                        OP("dve", "tensor_tensor_scan", HH[:, j, 0:sz], AA[:, j, 0:sz], IG[:, j, 0:sz], HST[:, c:c + 1],
                           op0=ALU.mult, op1=ALU.add, reads=[K_("AA", j), ("HST", c)] + igk, writes=[K_("RR", j)])
                        if tp > 0:
                            OP("dve", "tensor_copy", HST[:, c:c + 1], HH[:, j, tp - 1:tp], reads=[K_("RR", j)], writes=[("HST", c)])
                            if lastp_here:
                                OP("dve", "tensor_copy", OST[:, c, O_RGH_P:O_RGH_P + 1], HH[:, j, tp - 1:tp], reads=[K_("RR", j)], writes=[("OST", c)])
                        if hs:
                            OP("dve", "tensor_copy", OST[:, c, O_RGH_S:O_RGH_S + NS],
                               HH[:, j, sl:sl + NSC].rearrange("p (s t) -> p s t", t=ST)[:, :, ST - 1], reads=[K_("RR", j)], writes=[("OST", c)])
                        OP("dve", "tensor_tensor", G[:, c, o:o + sz], HH[:, j, 0:sz], GG[:, j, 0:sz], op=ALU.mult,
                           reads=[K_("RR", j), K_("GG", j)], writes=[("G", c, n)])

                return fb, ba, bd

            backs = []
            if "l0mix" not in skip and dbg >= 11:
                for bs_ in ([0, 1, 2, 3],):
                    sl_ = {}
                    for i, blk in enumerate(bs_):
                        sl_[blk] = next_weight(pending=i)
                    for n in allN:
                        if n == len(NT) - 1 and pend:
                            pend.pop()()
                        for bi_, blk in enumerate(bs_):
                            if len(NT) > 1 and n == len(NT) - 2 and bi_ == 2 and pend:
                                pend.pop()()
                            fb_, ba_, bd_ = rg_unit(blk, n, *sl_[blk])
                            if backs:
                                backs[0][0]()
                            fb_()
                            if backs:
                                backs.pop(0)[1]()
                            backs.append((ba_, bd_))
            def drain_backs():
                while backs:
                    ba_, bd_ = backs.pop(0)
                    ba_()
                    bd_()

            if "l0mix" not in skip and dbg >= 15:
                if len(NT) > 1:
                    pend = mixer_out(NT, R_NORM + 2 * 1 + 0, (R_NORM + 2 * 2 + 0) if "ffn0" not in skip else None,
                                     hooks={(0, 3): drain_backs})
                else:
                    drain_backs()
                    pend = mixer_out(NT, R_NORM + 2 * 1 + 0, (R_NORM + 2 * 2 + 0) if "ffn0" not in skip else None)
            drain_backs()
            pend = ffn(NT, 0, (R_NORM + 1) if "l1mix" not in skip else None, pend)

            def sc_unit(c, n, wv, wkeys):
                o, sz, tp, hs, sl, lastp_here = tile_info(n)
                bsx = unit_i[0] % 2
                B_ = SCS[bsx]
                unit_i[0] += 1
                BG, CV, CS, CC, CG = (B_[k] for k in ("BG", "CV", "CS", "CC", "CG"))
                K_ = lambda nm: (nm, bsx)
                banks = {}
                for nm, mo in (("cg", 128), ("v", 256), ("bg", 0)):
                    b = next_bank()
                    banks[nm] = b
                    mm_group(PS[:, b, 0:sz], [(wv[:, k, mo:mo + 128], HN[:, k, o:o + sz]) for k in range(KC)],
                             lambda i: wkeys + [("HN", i, n)], b)
                ACTF(CG[:, 0:sz], PS[:, banks["cg"], 0:sz], AF.Copy, [("PS", banks["cg"])], [K_("CG")])
                OP("dve", "tensor_tensor", CV[:, 2:2 + sz], CG[:, 0:sz], PS[:, banks["v"], 0:sz], op=ALU.mult,
                   reads=[K_("CG"), ("PS", banks["v"])], writes=[K_("CV")])
                ACTF(BG[:, 0:sz], PS[:, banks["bg"], 0:sz], AF.Copy, [("PS", banks["bg"])], [K_("BG")])
                cvk = [K_("CV")]
                OP("dve", "tensor_copy", CV[:, 0:2], SCST[:, c, 0:2], reads=[("SCST", c)], writes=[K_("CVpre")])
                if tp > 0:
                    ACTF(CC[:, 0:tp], CV[:, 2:2 + tp], AF.Identity, cvk + PSTK, [K_("CC")], scale=pcol(c, R_SCW + 2))
                    for k in (1, 0):
                        OP("dve", "scalar_tensor_tensor", CC[:, 0:tp], CV[:, k:k + tp], pcol(c, R_SCW + k), CC[:, 0:tp],
                           op0=ALU.mult, op1=ALU.add, reads=cvk + [K_("CVpre"), K_("CC")] + PSTK, writes=[K_("CC")])
                    OP("dve", "tensor_copy", SCST[:, c, 0:2], CV[:, tp:tp + 2], reads=cvk + [K_("CVpre")], writes=[("SCST", c)])
                    if lastp_here:
                        OP("dve", "tensor_copy", OST[:, c, O_SC_P:O_SC_P + 2], CV[:, tp:tp + 2], reads=cvk + [K_("CVpre")], writes=[("OST", c)])
                if hs:
                    OP("dve", "tensor_copy", CS[:, :, 0:2], PST[:, c, R_SC:R_SC + 32].rearrange("p (s t) -> p s t", t=2),
                       reads=PSTK, writes=[K_("CS")])
                    OP("dve", "tensor_copy", CS[:, :, 2:6], CV[:, 2 + sl:2 + sl + NSC].rearrange("p (s t) -> p s t", t=ST),
                       reads=cvk + [K_("CS")], writes=[K_("CS")])
                    ccs = CC[:, sl:sl + NSC].rearrange("p (s t) -> p s t", t=ST)
                    OP("dve", "tensor_scalar", ccs, CS[:, :, 2:6], pcol(c, R_SCW + 2), None, op0=ALU.mult,
                       reads=[K_("CS")] + PSTK, writes=[K_("CC")])
                    for k in (1, 0):
                        OP("dve", "scalar_tensor_tensor", ccs, CS[:, :, k:k + ST], pcol(c, R_SCW + k), ccs,
                           op0=ALU.mult, op1=ALU.add, reads=[K_("CS"), K_("CC")] + PSTK, writes=[K_("CC")])
                    OP("dve", "tensor_copy", OST[:, c, O_SC_S:O_SC_S + 32].rearrange("p (s t) -> p s t", t=2), CS[:, :, 4:6],
                       reads=[K_("CS")], writes=[("OST", c)])
                OP("dve", "tensor_tensor", G[:, c, o:o + sz], BG[:, 0:sz], CC[:, 0:sz], op=ALU.mult,
                   reads=[K_("BG"), K_("CC")], writes=[("G", c, n)])

            if "l1mix" not in skip:
                for cs_ in ([0, 1, 2, 3], [4], [5], [6], [7]):
                    sl_ = {}
                    for i, c in enumerate(cs_):
                        sl_[c] = next_weight(pending=i)
                    for n in allN:
                        if n == len(NT) - 1 and pend:
                            pend.pop()()
                        for c in cs_:
                            sc_unit(c, n, *sl_[c])
            if "l1mix" not in skip:
                pend = mixer_out(NT, R_NORM + 2 * 1 + 1, (R_NORM + 2 * 2 + 1) if "ffn1" not in skip else None)
            hk, aa = group_tail_hooks(g)
            pend = ffn(NT, 1, None, pend, hk, aa)
            assert not pend

        if dbg >= 6:
            store_rows(lambda c: OST[:, c, 0:NOUTST], NOUTST, T, lambda c: [("OST", c)])
        S.final_wait("sp")

        @block.tensor
        def _(e):
            S.emit("pe", e, esem, dsems)

        @block.scalar
        def _(e):
            S.emit("act", e, esem, dsems)

        @block.vector
        def _(e):
            S.emit("dve", e, esem, dsems)

        @block.gpsimd
        def _(e):
            S.emit("pool", e, esem, dsems)

        @block.sync
        def _(e):
            S.emit("sp", e, esem, dsems)
    return nc


def make_core_inputs(c, x_prompt, x_sample, state_rglru_conv, state_rglru_h, state_sconv, meta_tokens,
                     norm_mix_pre, norm_mix_post, norm_ffn_pre, norm_ffn_post, rg_conv_w, rg_conv_b,
                     rg_gate_a_b, rg_gate_x_b, rg_lambda, sc_conv_w):
    f = lambda a: np.asarray(a, dtype=np.float32)
    s0, s1 = NS * c, NS * (c + 1)
    xin = np.concatenate([f(meta_tokens), f(x_prompt[c]), f(x_sample[s0:s1]).reshape(NSC, D)], axis=0)
    pst = np.concatenate([
        f(state_rglru_conv[0, s0:s1]).reshape(NS * 3, D),
        f(state_rglru_h[0, s0:s1]).reshape(NS, D),
        f(state_sconv[0, s0:s1]).reshape(NS * 2, D),
        f(norm_mix_pre), f(norm_mix_post), f(norm_ffn_pre), f(norm_ffn_post),
        f(rg_conv_w[0]), f(rg_conv_b), f(rg_gate_a_b).reshape(1, D), f(rg_gate_x_b).reshape(1, D),
        f(rg_lambda), f(sc_conv_w[0]),
    ], axis=0)
    assert pst.shape == (NPST, D), pst.shape
    return np.ascontiguousarray(xin), np.ascontiguousarray(pst)


_NC_CACHE = {}


def kernel(x_prompt, x_sample, state_rglru_conv, state_rglru_h, state_sconv, meta_tokens,
           norm_mix_pre, norm_mix_post, norm_ffn_pre, norm_ffn_post,
           rg_w_in, rg_conv_w, rg_conv_b, rg_gate_a_w, rg_gate_a_b, rg_gate_x_w, rg_gate_x_b,
           rg_lambda, rg_w_out, sc_w_in, sc_conv_w, sc_w_out,
           ffn_w_gate, ffn_w_up, ffn_w_down):
    f = lambda a: np.ascontiguousarray(np.asarray(a, dtype=np.float32))
    B, SEQ, _ = x_prompt.shape
    TP = NMETA + SEQ
    T = TP + NSC
    assert B == N_CORES and x_sample.shape[0] == NS * N_CORES and x_sample.shape[1] == ST
    if TP not in _NC_CACHE:
        _NC_CACHE[TP] = build(TP)
    nc = _NC_CACHE[TP]
    shared = {
        "ident": np.eye(128, dtype=np.float32),
        "rg_w_in": f(rg_w_in[0]), "ga_w": f(rg_gate_a_w[0]), "gx_w": f(rg_gate_x_w[0]),
        "rg_w_out": f(rg_w_out[0]), "sc_w_in": f(sc_w_in[0]), "sc_w_out": f(sc_w_out[0]),
        "w_gate": f(ffn_w_gate), "w_up": f(ffn_w_up), "w_down": f(ffn_w_down),
    }
    in_maps = []
    for c in range(N_CORES):
        xin, pst = make_core_inputs(c, x_prompt, x_sample, state_rglru_conv, state_rglru_h, state_sconv, meta_tokens,
                                    norm_mix_pre, norm_mix_post, norm_ffn_pre, norm_ffn_post, rg_conv_w, rg_conv_b,
                                    rg_gate_a_b, rg_gate_x_b, rg_lambda, sc_conv_w)
        m = dict(shared)
        m["xin"] = xin
        m["pst"] = pst
        in_maps.append(m)
    res = run_bass_kernel_spmd(nc, in_maps, core_ids=list(range(N_CORES)))
    outs = [np.asarray(r["yout"], dtype=np.float32) for r in res.results]
    y_prompt = np.stack([o[NMETA:TP] for o in outs], axis=0)
    y_sample = np.concatenate([o[TP:T].reshape(NS, ST, D) for o in outs], axis=0)
    st = [o[T:T + NOUTST] for o in outs]
    rg_conv_p = np.stack([s[O_RGC_P:O_RGC_P + 3] for s in st], axis=0)[None]
    rg_h_p = np.stack([s[O_RGH_P] for s in st], axis=0)[None]
    sc_p = np.stack([s[O_SC_P:O_SC_P + 2] for s in st], axis=0)[None]
    rg_conv_s = np.concatenate([s[O_RGC_S:O_RGC_S + 48].reshape(NS, 3, D) for s in st], axis=0)[None]
    rg_h_s = np.concatenate([s[O_RGH_S:O_RGH_S + NS] for s in st], axis=0)[None]
    sc_s = np.concatenate([s[O_SC_S:O_SC_S + 32].reshape(NS, 2, D) for s in st], axis=0)[None]
    c = np.ascontiguousarray
    return (c(y_prompt), c(y_sample), c(rg_conv_p), c(rg_h_p), c(sc_p), c(rg_conv_s), c(rg_h_s), c(sc_s))
```

```python
import numpy as np
import concourse.bass as bass
import concourse.mybir as mybir
from concourse.bass_utils import run_bass_kernel_spmd

F32 = mybir.dt.float32
BF16 = mybir.dt.bfloat16
AF = mybir.ActivationFunctionType
ALU = mybir.AluOpType

D = 1024
KC = 8
DFF = 2816
FC = 22
NMETA = 16
EPS = 1e-6
NS = 16
ST = 4
NSC = NS * ST
N_CORES = 8

R_CONV, R_H, R_SC, R_NORM, R_RGCW, R_RGCB, R_GAB, R_GXB, R_LAM, R_SCW = 0, 48, 64, 96, 104, 108, 109, 110, 111, 112
NPST = 115
O_RGC_P, O_RGH_P, O_SC_P, O_RGC_S, O_RGH_S, O_SC_S = 0, 3, 4, 6, 54, 70
NOUTST = 102
LNHALF = -0.6931471805599453

ENGS = ("pe", "act", "dve", "pool", "sp")
NSLOT = 6
SLOT_ELEMS = 4096
N_DSEM = 16


class Sched:
    def __init__(self):
        self.q = {e: [] for e in ENGS}
        self.cnt = {e: 0 for e in ENGS}
        self.seen = {e: {} for e in ENGS}
        self.res = {}
        self.dval = [0] * N_DSEM

    def _deps(self, eng, reads, writes):
        deps = {}

        def need(k, v):
            if k == "pe" and eng == "pe":
                return
            if deps.get(k, 0) < v:
                deps[k] = v

        for r in reads:
            e = self.res.get(r)
            if e and e[0]:
                need(*e[0])
        for w in writes:
            e = self.res.get(w)
            if e:
                if e[0]:
                    need(*e[0])
                for k, v in e[1].items():
                    need(k, v)
        out = []
        for k, v in deps.items():
            if self.seen[eng].get(k, 0) < v:
                self.seen[eng][k] = v
                out.append((k, v))
        return out

    def _record(self, tok, reads, writes):
        for r in reads:
            e = self.res.setdefault(r, [None, {}])
            if e[1].get(tok[0], 0) < tok[1]:
                e[1][tok[0]] = tok[1]
        for w in writes:
            self.res[w] = [tok, {}]

    def op(self, eng, fn, reads=(), writes=(), inc=True):
        waits = self._deps(eng, reads, writes)
        tok = (eng, self.cnt[eng] + 1)
        if inc:
            self.cnt[eng] += 1
        self._record(tok, reads, writes)
        self.q[eng].append((waits, fn, eng if inc else None, False))
        return tok

    def dma(self, eng, fn, dsem, reads=(), writes=(), final=None):
        waits = self._deps(eng, reads, writes)
        self.dval[dsem] += 16
        tok = (("d", dsem), final if final is not None else self.dval[dsem])
        self._record(tok, reads, writes)
        self.q[eng].append((waits, fn, ("d", dsem), True))
        return tok

    def barrier(self, engs=("pe", "act", "dve")):
        for e in engs:
            waits = []
            for o in engs:
                if (o != e or e != "pe") and self.seen[e].get(o, 0) < self.cnt[o]:
                    self.seen[e][o] = self.cnt[o]
                    waits.append((o, self.cnt[o]))
            if waits:
                self.q[e].append((waits, None, None, False))

    def final_wait(self, eng):
        waits = []
        for e in ("pe", "act", "dve", "pool"):
            if e != eng and self.cnt[e] > 0:
                waits.append((e, self.cnt[e]))
        for i, v in enumerate(self.dval):
            if v > 0:
                waits.append((("d", i), v))
        self.q[eng].append((waits, None, None, False))

    def emit(self, name, eng, esem, dsems):
        def sem_of(k):
            return dsems[k[1]] if isinstance(k, tuple) else esem[k]

        for waits, fn, upd, is_dma in self.q[name]:
            sems = [(sem_of(k), v) for k, v in waits]
            if fn is None:
                for s, v in sems:
                    eng.wait_ge(s, v)
                continue
            attach = None
            if sems and not is_dma:
                attach = sems[0]
                sems = sems[1:]
            for s, v in sems:
                eng.wait_ge(s, v)
            ins = fn(eng)
            if attach is not None:
                ins._wait_ge(attach[0], attach[1])
            if upd is not None:
                if is_dma:
                    ins.then_inc(dsems[upd[1]], 16)
                else:
                    ins.then_inc(esem[upd], 1)


def split_tiles(n, maxw=512):
    k = (n + maxw - 1) // maxw
    if n > 256 and k < 2:
        k = 2
    base, rem = divmod(n, k)
    out, o = [], 0
    for i in range(k):
        s = base + (1 if i < rem else 0)
        out.append((o, s))
        o += s
    return out


def make_groups(T, ng):
    base, rem = divmod(T, ng)
    out, o = [], 0
    for i in range(ng):
        s = base + (1 if i < rem else 0)
        out.append((o, o + s))
        o += s
    return out


def build(TP, NG=3, skip=(), dbg=99):
    T = TP + NSC
    groups = make_groups(T, NG)
    TG = max(c1 - c0 for c0, c1 in groups)
    assert groups[-1][0] < TP - 3 and all(c1 <= TP for c0, c1 in groups[:-1]) and all(min(c1, TP) - c0 >= 3 for c0, c1 in groups)
    NTW = max(s for c0, c1 in groups for o, s in split_tiles(c1 - c0))

    nc = bass.Bass("TRN2", target_bir_lowering=False)
    dr = lambda n, s: nc.dram_tensor(n, s, F32, kind="ExternalInput").ap()
    xin = dr("xin", [T, D])
    pst = dr("pst", [NPST, D])
    ident_d = dr("ident", [128, 128])
    rg_w_in = dr("rg_w_in", [D, 2 * D])
    ga_w = dr("ga_w", [4, 256, 256])
    gx_w = dr("gx_w", [4, 256, 256])
    rg_w_out = dr("rg_w_out", [D, D])
    sc_w_in = dr("sc_w_in", [D, 3 * D])
    sc_w_out = dr("sc_w_out", [D, D])
    w_gate = dr("w_gate", [2, D, DFF])
    w_up = dr("w_up", [2, D, DFF])
    w_down = dr("w_down", [2, DFF, D])
    yout = nc.dram_tensor("yout", [T + NOUTST, D], F32, kind="ExternalOutput").ap()

    from contextlib import ExitStack
    S = Sched()
    with ExitStack() as es:
        sb = lambda n, s, dt=F32: es.enter_context(nc.sbuf_tensor(n, s, dt))
        X = sb("X", [128, KC, TG])
        HN = sb("HN", [128, KC, TG], BF16)
        G = sb("G", [128, KC, TG], BF16)
        SQ = sb("SQ", [128, KC, TG], BF16)
        Y = sb("Y", [128, KC, TG])
        RSTD = sb("RSTD", [128, TG])
        ARENA_B = max(FC * TG * 2, 2 * ((2 * (NTW + 3) + 2 * NS * 7 + 10 * NTW) * 4 + (2 * NTW + 2) * 2) + 64)
        ARENA_B = (ARENA_B + 63) // 64 * 64
        AR = sb("AR", [128, ARENA_B // 4])
        WR = sb("WR", [128, NSLOT, SLOT_ELEMS], BF16)
        GW = sb("GW", [128, 2, 4, 2, 256], BF16)
        INB = sb("INB", [128, 2, D])
        OUTB = sb("OUTB", [128, 2, D])
        PST = sb("PST", [128, KC, 128])
        OST = sb("OST", [128, KC, 128])
        IDENT = sb("IDENT", [128, 128])
        ONES = sb("ONES", [128, 128], BF16)
        CONVST = sb("CONVST", [128, KC, 4])
        SCST = sb("SCST", [128, KC, 2])
        HST = sb("HST", [128, KC])
        CNEG = sb("CNEG", [128, KC])
        C2NEG = sb("C2NEG", [128, KC])
        TMPV = sb("TMPV", [128, KC])
        HC = sb("HC", [128, KC])
        HGB = sb("HGB", [128, 2, KC])
        TMP16 = sb("TMP16", [128, 2, NS])
        SIL = sb("SIL", [128, 2, NTW])
        PS = es.enter_context(nc.psum_tensor("PS", [128, 8, 512], F32))
        esem = {e: es.enter_context(nc.semaphore("s_" + e)) for e in ENGS}
        dsems = [es.enter_context(nc.semaphore("d%d" % i)) for i in range(N_DSEM)]
        block = es.enter_context(nc.Block())

        arena_f32 = AR[:, :]
        arena_bf = AR[:, :].bitcast(BF16)
        ACTB = arena_bf[:, 0:FC * TG].rearrange("p (j t) -> p j t", t=TG)
        off = [0]

        def carve(nelem, dt=F32, shape=None):
            o = off[0]
            if dt == F32:
                v = arena_f32[:, o:o + nelem]
                off[0] += nelem
            else:
                v = arena_bf[:, 2 * o:2 * o + nelem]
                off[0] += (nelem + 1) // 2
            return v

        W_ = NTW
        off[0] = 0
        RGS = []
        for i in range(2):
            d = {}
            d["XR"] = carve(2 * (W_ + 3)).rearrange("p (j t) -> p j t", t=W_ + 3)
            d["XRS"] = carve(2 * NS * 7).rearrange("p (j s t) -> p j s t", s=NS, t=7)
            for nm in ("GG", "XC", "RR", "IG", "AA"):
                d[nm] = carve(2 * W_).rearrange("p (j t) -> p j t", t=W_)
            d["HH"] = d["RR"]
            d["XCB"] = carve(2 * W_ + 2, BF16)[:, 0:2 * W_].rearrange("p (j t) -> p j t", t=W_)
            RGS.append(d)
        rg_words = off[0]
        off[0] = 0
        SCS = []
        for i in range(2):
            d = {}
            d["BG"] = carve(W_)
            d["CV"] = carve(W_ + 2)
            d["CS"] = carve(NS * 6).rearrange("p (s t) -> p s t", t=6)
            d["CC"] = carve(W_)
            d["CG"] = carve(W_)
            SCS.append(d)
        assert max(rg_words, off[0]) * 4 <= ARENA_B, (rg_words, off[0], ARENA_B)

        def OP(eng, method, *args, reads=(), writes=(), inc=True, **kw):
            return S.op(eng, lambda e: getattr(e, method)(*args, **kw), reads, writes, inc)

        def DMA(eng, dsem, out, in_, reads=(), writes=(), final=None):
            return S.dma(eng, lambda e: e.dma_start(out=out, in_=in_), dsem, reads, writes, final)

        def ACTF(out, in_, func, reads, writes, **kw):
            return OP("act", "activation", out=out, in_=in_, func=func, reads=reads, writes=writes, **kw)

        bank_i = [0]

        def next_bank():
            b = bank_i[0] % 8
            bank_i[0] += 1
            return b

        def mm_group(out_ap, pairs, reads_of, bank):
            n = len(pairs)
            for i, (lhsT, rhs) in enumerate(pairs):
                OP("pe", "matmul", out_ap, lhsT, rhs, start=(i == 0), stop=(i == n - 1),
                   reads=reads_of(i), writes=[("PS", bank)], inc=(i == n - 1))

        def plan_weights():
            plan = []
            for g in range(NG):
                nnt = len(split_tiles(groups[g][1] - groups[g][0]))
                for b in range(4 if ("l0mix" not in skip and dbg >= 11) else 0):
                    plan.append((8, 512, [(rg_w_in, 256 * b, 256, 0), (rg_w_in, D + 256 * b, 256, 256)]))
                for q in range(2 if ("l0mix" not in skip and dbg >= 15) else 0):
                    plan.append((8, 512, [(rg_w_out, 512 * q, 512, 0)]))
                for L in range(2):
                    if L == 1 and "l1mix" not in skip:
                        for c in range(KC):
                            plan.append((8, 384, [(sc_w_in, 128 * c, 128, 0), (sc_w_in, D + 128 * c, 128, 128),
                                                  (sc_w_in, 2 * D + 128 * c, 128, 256)]))
                        for q in range(2):
                            plan.append((8, 512, [(sc_w_out, 512 * q, 512, 0)]))
                    if ("ffn%d" % L) in skip:
                        continue
                    for q in range(6):
                        w = min(512, DFF - 512 * q)
                        plan.append((8, w, [(w_gate[L], 512 * q, w, 0)]))
                        plan.append((8, w, [(w_up[L], 512 * q, w, 0)]))
                    for n_ in range(nnt):
                        for m in range(KC):
                            plan.append((FC, 128, [(w_down[L], 128 * m, 128, 0)]))
            return plan

        wstate = {"next_use": 0, "issued": 0, "plan": plan_weights()}

        def wslot_view(u):
            K, width, parts = wstate["plan"][u]
            s = u % NSLOT
            return WR[:, s, 0:K * width].rearrange("p (k n) -> p k n", n=width)

        def issue_weight(u):
            K, width, parts = wstate["plan"][u]
            s = u % NSLOT
            view = wslot_view(u)
            fin = S.dval[s] + 16 * len(parts)
            for pi, (w, c0, wd, doff) in enumerate(parts):
                src = w.rearrange("(k p) n -> p k n", p=128)[:, :, c0:c0 + wd]
                DMA("pool", s, view[:, :, doff:doff + wd], src, writes=[("W", s, pi)], final=fin)

        def next_weight(pending=0):
            u = wstate["next_use"]
            wstate["next_use"] += 1
            while wstate["issued"] < min(len(wstate["plan"]), u + NSLOT - pending):
                issue_weight(wstate["issued"])
                wstate["issued"] += 1
            s = u % NSLOT
            return wslot_view(u), [("W", s, 0), ("W", s, 1), ("W", s, 2)]

        DMA("sp", 12, IDENT[:, :], ident_d[:, :], writes=["IDENT"])
        DMA("sp", 13, INB[0:NPST, 0, :], pst[:, :], writes=[("INB", 0)])
        OP("dve", "memset", ONES[:, :], 1.0, writes=["ONES"])
        OP("dve", "memset", CONVST[:, :, :], 0.0, writes=[("CONVST", c) for c in range(KC)])
        OP("dve", "memset", SCST[:, :, :], 0.0, writes=[("SCST", c) for c in range(KC)])
        OP("dve", "memset", HST[:, :], 0.0, writes=[("HST", c) for c in range(KC)])
        OP("dve", "memset", OST[:, :, :], 0.0, writes=[("OST", c) for c in range(KC)])
        for t, gwd in enumerate((ga_w, gx_w) if dbg >= 2 else ()):
            for b in range(4):
                DMA("pool", 6 + t, GW[:, t, b, :, :], gwd[b].rearrange("(ki p) n -> p ki n", p=128), writes=[("GW", t, b)], final=64)
        GWK = [[("GW", t, b) for b in range(4)] for t in range(2)]

        def transpose_in(slot, rows, dst_of_half, keys_of_half, alt=0):
            for half in range(2):
                b = next_bank()
                for q in range(4):
                    c = half * 4 + q
                    OP("pe", "transpose", PS[:, b, q * 128:q * 128 + rows], INB[0:rows, slot, c * 128:(c + 1) * 128],
                       IDENT[0:rows, 0:rows], reads=[("INB", slot), "IDENT"], writes=[("PS", b)], inc=(q == 3))
                src = PS[:, b, :].rearrange("p (q r) -> p q r", r=128)[:, :, 0:rows]
                if (half + alt) % 2 == 0:
                    OP("dve", "tensor_copy", dst_of_half(half), src, reads=[("PS", b)], writes=keys_of_half(half))
                else:
                    ACTF(dst_of_half(half), src, AF.Copy, [("PS", b)], keys_of_half(half))

        if dbg >= 3:
            transpose_in(0, NPST, lambda h: PST[:, h * 4:(h + 1) * 4, 0:NPST], lambda h: [("PST", h)])
        PSTK = [("PST", 0), ("PST", 1)]
        if dbg >= 4:
            ACTF(TMPV[:, :], PST[:, :, R_LAM], AF.Exp, PSTK, ["TMPV"], scale=-1.0)
            ACTF(TMPV[:, :], TMPV[:, :], AF.Ln, ["TMPV"], ["TMPV"], bias=1.0)
            OP("dve", "tensor_scalar", CNEG[:, :], TMPV[:, :], -8.0, None, op0=ALU.mult, reads=["TMPV"], writes=["CNEG"])
            OP("dve", "tensor_scalar", C2NEG[:, :], TMPV[:, :], -16.0, None, op0=ALU.mult, reads=["TMPV"], writes=["C2NEG"])
            OP("dve", "tensor_scalar", HC[:, :], TMPV[:, :], -4.0, None, op0=ALU.mult, reads=["TMPV"], writes=["HC"])
            OP("dve", "tensor_scalar", HGB[:, 0, :], PST[:, :, R_GAB], 0.5, None, op0=ALU.mult, reads=PSTK, writes=["HGB0"])
            OP("dve", "tensor_scalar", HGB[:, 1, :], PST[:, :, R_GXB], 0.5, None, op0=ALU.mult, reads=PSTK, writes=["HGB1"])

        def pcol(c, r):
            return PST[:, c, r:r + 1]

        in_slot = [1]
        out_slot = [0]

        def norm_rstd(n, o, sz):
            b = next_bank()
            mm_group(PS[:, b, 0:sz], [(ONES[:, :], SQ[:, c, o:o + sz]) for c in range(KC)],
                     lambda i: [("SQ", i, n), "ONES"], b)
            ACTF(RSTD[:, o:o + sz], PS[:, b, 0:sz], AF.Ln, [("PS", b)], [("RSTD", n)], scale=1.0 / D, bias=EPS)
            ACTF(RSTD[:, o:o + sz], RSTD[:, o:o + sz], AF.Exp, [("RSTD", n)], [("RSTD", n)], scale=-0.5)

        def prenorm_sq(n, o, sz):
            ACTF(SQ[:, :, o:o + sz], X[:, :, o:o + sz], AF.Square,
                 [("X", c, n) for c in range(KC)], [("SQ", c, n) for c in range(KC)])

        def prenorm_n(n, o, sz, nrow, do_sq=True):
            if do_sq:
                prenorm_sq(n, o, sz)
            norm_rstd(n, o, sz)
            for c in range(KC):
                OP("dve", "scalar_tensor_tensor", HN[:, c, o:o + sz], X[:, c, o:o + sz], pcol(c, nrow), RSTD[:, o:o + sz],
                   op0=ALU.mult, op1=ALU.mult, reads=[("X", c, n), ("RSTD", n)] + PSTK, writes=[("HN", c, n)])

        def prenorm(NT, nrow):
            for n, (o, sz) in enumerate(NT):
                prenorm_n(n, o, sz, nrow)

        def postnorm_n(n, o, sz, nrow):
            norm_rstd(n, o, sz)
            for c in range(KC):
                OP("dve", "scalar_tensor_tensor", Y[:, c, o:o + sz], Y[:, c, o:o + sz], pcol(c, nrow), RSTD[:, o:o + sz],
                   op0=ALU.mult, op1=ALU.mult, reads=[("Y", c, n), ("RSTD", n)] + PSTK, writes=[("Y", c, n)])
                OP("dve", "tensor_tensor", X[:, c, o:o + sz], X[:, c, o:o + sz], Y[:, c, o:o + sz], op=ALU.add,
                   reads=[("X", c, n), ("Y", c, n)], writes=[("X", c, n)])

        def proj_out(NT, src, srckey, nk, wview, post_row, next_pre_row, mh, hooks=None, after_all=None):
            for n, (o, sz) in enumerate(NT):
                for m in range(KC):
                    wv, wkeys, mo = wview(m, n)
                    b = next_bank()
                    mm_group(PS[:, b, 0:sz], [(wv[:, k, mo:mo + 128], src[:, k, o:o + sz]) for k in range(nk)],
                             lambda i: wkeys + [(srckey, i, n)], b)
                    ACTF(Y[:, m, o:o + sz], PS[:, b, 0:sz], AF.Copy, [("PS", b)], [("Y", m, n)])
                    ACTF(SQ[:, m, o:o + sz], PS[:, b, 0:sz], AF.Square, [("PS", b)], [("SQ", m, n)])
                    if next_pre_row is not None and n > 0 and m == mh:
                        po, psz = NT[n - 1]
                        prenorm_n(n - 1, po, psz, next_pre_row)
                    if hooks and (n, m) in hooks:
                        hooks[(n, m)]()
                postnorm_n(n, o, sz, post_row)
            if after_all:
                after_all()
            if next_pre_row is not None:
                po, psz = NT[-1]
                return [lambda: prenorm_n(len(NT) - 1, po, psz, next_pre_row)]
            return []

        def mixer_out(NT, post_row, next_pre_row, hooks=None):
            v0, k0 = next_weight()
            v1, k1 = next_weight(pending=1)
            views = [(v0, k0), (v1, k1)]
            return proj_out(NT, G, "G", KC, lambda m, n: (views[m // 4][0], views[m // 4][1], (m % 4) * 128), post_row, next_pre_row, 6, hooks)

        def ffn(NT, L, next_pre_row, pend, hooks=None, after_all=None):
            if ("ffn%d" % L) in skip:
                if after_all:
                    after_all()
                return pend
            for qs in ([0, 1], [2], [3], [4], [5]):
                slots = {}
                for i, q in enumerate(qs):
                    slots[q] = (next_weight(pending=2 * i), next_weight(pending=2 * i + 1))
                for n, (o, sz) in enumerate(NT):
                    if n == len(NT) - 1 and pend:
                        pend.pop()()
                    for q in qs:
                        w = min(512, DFF - 512 * q)
                        (wg, kg), (wu, ku) = slots[q]
                        for jj in range(w // 128):
                            j = 4 * q + jj
                            bg_ = next_bank()
                            bu_ = next_bank()
                            for (bb, wv, wk) in ((bg_, wg, kg), (bu_, wu, ku)):
                                mm_group(PS[:, bb, 0:sz], [(wv[:, k, jj * 128:(jj + 1) * 128], HN[:, k, o:o + sz]) for k in range(KC)],
                                         lambda i, wk=wk: wk + [("HN", i, n)], bb)
                            si = (j * len(NT) + n) % 2
                            ACTF(SIL[:, si, 0:sz], PS[:, bg_, 0:sz], AF.Silu, [("PS", bg_)], [("SIL", si)])
                            OP("dve", "tensor_tensor", ACTB[:, j, o:o + sz], SIL[:, si, 0:sz], PS[:, bu_, 0:sz], op=ALU.mult,
                               reads=[("SIL", si), ("PS", bu_)], writes=[("ACTB", j, n)])

            def wview(m, n):
                v, k = next_weight()
                return v, k, 0

            return proj_out(NT, ACTB, "ACTB", FC, wview, R_NORM + 2 * 3 + L, next_pre_row, 3, hooks, after_all)

        def store_rows(src_of_c, rows, drow0, keys_of_c):
            slot = out_slot[0] % 2
            out_slot[0] += 1
            for half in range(2):
                b = next_bank()
                for q in range(4):
                    c = half * 4 + q
                    OP("pe", "transpose", PS[0:rows, b, q * 128:(q + 1) * 128], src_of_c(c), IDENT[:, :],
                       reads=["IDENT"] + keys_of_c(c), writes=[("PS", b)], inc=(q == 3))
                if half == 0:
                    OP("dve", "tensor_copy", OUTB[0:rows, slot, 0:512], PS[0:rows, b, :], reads=[("PS", b)], writes=[("OUTB", slot, 0)])
                else:
                    ACTF(OUTB[0:rows, slot, 512:1024], PS[0:rows, b, :], AF.Copy, [("PS", b)], [("OUTB", slot, 1)])
            DMA("sp", 8 + slot, yout[drow0:drow0 + rows, :], OUTB[0:rows, slot, :], reads=[("OUTB", slot, 0), ("OUTB", slot, 1)])

        NTs = [split_tiles(c1_ - c0_) for (c0_, c1_) in groups]

        def emit_load(g, n):
            c0g = groups[g][0]
            o, sz = NTs[g][n]
            for lo in range(o, o + sz, 128):
                rows = min(128, o + sz - lo)
                slot = in_slot[0] % 2
                in_slot[0] += 1
                DMA("sp", 10 + slot, INB[0:rows, slot, :], xin[c0g + lo:c0g + lo + rows, :], writes=[("INB", slot)])
                transpose_in(slot, rows, lambda h: X[:, h * 4:(h + 1) * 4, lo:lo + rows],
                             lambda h: [("X", c, n) for c in range(4 * h, 4 * h + 4)], alt=(lo // 128))

        def emit_store(g, n):
            c0g = groups[g][0]
            o, sz = NTs[g][n]
            for lo in range(o, o + sz, 128):
                rows = min(128, o + sz - lo)
                store_rows(lambda c: X[:, c, lo:lo + rows], rows, c0g + lo, lambda c: [("X", c, n)])

        def emit_l0_pre(g, n):
            if "l0mix" not in skip:
                prenorm_n(n, NTs[g][n][0], NTs[g][n][1], R_NORM + 0)

        def group_tail_hooks(g):
            L_ = len(NTs[g])
            nxt = g + 1 if g + 1 < len(groups) else None
            Ln = len(NTs[nxt]) if nxt is not None else 0
            hooks = {}
            for n in range(1, L_):
                hooks[(n, 3)] = (lambda n=n: emit_store(g, n - 1))
                if nxt is not None and n - 1 < Ln:
                    hooks[(n, 5)] = (lambda n=n: emit_load(nxt, n - 1))
                    if n - 1 < Ln - 1:
                        hooks[(n, 6)] = (lambda n=n: emit_l0_pre(nxt, n - 1))

            def after_all():
                emit_store(g, L_ - 1)
                if nxt is not None:
                    for n2 in range(L_ - 1, Ln):
                        emit_load(nxt, n2)
                        if n2 < Ln - 1:
                            emit_l0_pre(nxt, n2)
            return hooks, after_all

        for g, (c0, c1) in enumerate(groups if dbg >= 5 else ()):
            Tg = c1 - c0
            Tp = min(c1, TP) - c0
            has_s = c1 > TP
            so = Tp
            last_p = (min(c1, TP) == TP)
            NT = split_tiles(Tg)
            allN = list(range(len(NT)))

            if g == 0:
                for n_ in allN:
                    emit_load(0, n_)
                    if n_ < len(NT) - 1:
                        emit_l0_pre(0, n_)

            def tile_info(n):
                o, sz = NT[n]
                tp = max(0, min(Tp - o, sz))
                assert tp == 0 or tp >= 3
                hs = has_s and (o <= so < o + sz)
                if has_s:
                    assert (o <= so and so + NSC <= o + sz) or (so + NSC <= o or so >= o + sz)
                return o, sz, tp, hs, (so - o), (last_p and tp > 0 and o + tp == Tp)

            pend = []
            if "l0mix" not in skip:
                sq_done = {"v": False}
                pend = [lambda: prenorm_n(len(NT) - 1, NT[-1][0], NT[-1][1], R_NORM + 0, do_sq=not sq_done["v"])]
            unit_i = [0]

            def rg_unit(blk, n, wv, wkeys):
                o, sz, tp, hs, sl, lastp_here = tile_info(n)
                B_ = RGS[unit_i[0] % 2]
                bsx = unit_i[0] % 2
                unit_i[0] += 1
                XR, XRS, GG, XC, RR, IG, AA, HH, XCB = (B_[k] for k in ("XR", "XRS", "GG", "XC", "RR", "IG", "AA", "HH", "XCB"))
                K_ = lambda nm, j: (nm, bsx, j)
                for mi in (2, 3, 0, 1):
                    j = mi % 2
                    b = next_bank()
                    mm_group(PS[:, b, 0:sz], [(wv[:, k, mi * 128:(mi + 1) * 128], HN[:, k, o:o + sz]) for k in range(KC)],
                             lambda i: wkeys + [("HN", i, n)], b)
                    if mi >= 2:
                        OP("dve", "tensor_copy", XR[:, j, 3:3 + sz], PS[:, b, 0:sz], reads=[("PS", b)], writes=[K_("XR", j)])
                    else:
                        ACTF(GG[:, j, 0:sz], PS[:, b, 0:sz], AF.Gelu_apprx_tanh, [("PS", b)], [K_("GG", j)])
                for j in range(2):
                    c = 2 * blk + j
                    xrk = [K_("XR", j)]
                    OP("dve", "tensor_copy", XR[:, j, 0:3], CONVST[:, c, 0:3], reads=[("CONVST", c)], writes=[K_("XRpre", j)])
                    if tp > 0:
                        OP("dve", "tensor_scalar", XC[:, j, 0:tp], XR[:, j, 3:3 + tp], pcol(c, R_RGCW + 3), pcol(c, R_RGCB),
                           op0=ALU.mult, op1=ALU.add, reads=xrk + PSTK, writes=[K_("XC", j)])
                        for k in range(3):
                            OP("dve", "scalar_tensor_tensor", XC[:, j, 0:tp], XR[:, j, k:k + tp], pcol(c, R_RGCW + k), XC[:, j, 0:tp],
                               op0=ALU.mult, op1=ALU.add, reads=xrk + [K_("XRpre", j), K_("XC", j)] + PSTK, writes=[K_("XC", j)])
                        OP("dve", "tensor_copy", CONVST[:, c, 0:3], XR[:, j, tp:tp + 3], reads=xrk + [K_("XRpre", j)], writes=[("CONVST", c)])
                        if lastp_here:
                            OP("dve", "tensor_copy", OST[:, c, O_RGC_P:O_RGC_P + 3], XR[:, j, tp:tp + 3],
                               reads=xrk + [K_("XRpre", j)], writes=[("OST", c)])
                    if hs:
                        OP("dve", "tensor_copy", XRS[:, j, :, 0:3], PST[:, c, R_CONV:R_CONV + 48].rearrange("p (s t) -> p s t", t=3),
                           reads=PSTK, writes=[K_("XRS", j)])
                        OP("dve", "tensor_copy", XRS[:, j, :, 3:7], XR[:, j, 3 + sl:3 + sl + NSC].rearrange("p (s t) -> p s t", t=ST),
                           reads=xrk + [K_("XRS", j)], writes=[K_("XRS", j)])
                        xcs = XC[:, j, sl:sl + NSC].rearrange("p (s t) -> p s t", t=ST)
                        OP("dve", "tensor_scalar", xcs, XRS[:, j, :, 3:7], pcol(c, R_RGCW + 3), pcol(c, R_RGCB),
                           op0=ALU.mult, op1=ALU.add, reads=[K_("XRS", j)] + PSTK, writes=[K_("XC", j)])
                        for k in range(3):
                            OP("dve", "scalar_tensor_tensor", xcs, XRS[:, j, :, k:k + ST], pcol(c, R_RGCW + k), xcs,
                               op0=ALU.mult, op1=ALU.add, reads=[K_("XRS", j), K_("XC", j)] + PSTK, writes=[K_("XC", j)])
                        OP("dve", "tensor_copy", OST[:, c, O_RGC_S:O_RGC_S + 48].rearrange("p (s t) -> p s t", t=3), XRS[:, j, :, 4:7],
                           reads=[K_("XRS", j)], writes=[("OST", c)])

                def fb():
                    ACTF(XCB[:, :, 0:sz], XC[:, :, 0:sz], AF.Copy, [K_("XC", 0), K_("XC", 1)], [K_("XCB", 0), K_("XCB", 1)])

                def ba():
                    for t in range(2):
                        for mo in range(2):
                            c = 2 * blk + mo
                            b = next_bank()
                            mm_group(PS[:, b, 0:sz], [(GW[:, t, blk, ki, mo * 128:(mo + 1) * 128], XCB[:, ki, 0:sz]) for ki in range(2)],
                                     lambda i: GWK[t] + [K_("XCB", i)], b)
                            dstb = RR if t == 0 else IG
                            ACTF(dstb[:, mo, 0:sz], PS[:, b, 0:sz], AF.Tanh, [("PS", b), "HGB%d" % t],
                                 [K_("RR" if t == 0 else "IG", mo)], scale=0.5, bias=HGB[:, t, c:c + 1])
                    for j in range(2):
                        c = 2 * blk + j
                        ACTF(AA[:, j, 0:sz], RR[:, j, 0:sz], AF.Exp, [K_("RR", j), "HC"], [K_("AA", j)],
                             scale=HC[:, c:c + 1], bias=HC[:, c:c + 1])
                    rrk2 = [K_("RR", 0), K_("RR", 1)]
                    aak2 = [K_("AA", 0), K_("AA", 1)]
                    ACTF(RR[:, :, 0:sz], AA[:, :, 0:sz], AF.Square, aak2 + rrk2, rrk2)
                    ACTF(RR[:, :, 0:sz], RR[:, :, 0:sz], AF.Ln, rrk2, rrk2, scale=-1.0, bias=1.0)
                    ACTF(RR[:, :, 0:sz], RR[:, :, 0:sz], AF.Exp, rrk2, rrk2, scale=0.5, bias=LNHALF)

                def bd():
                    for j in range(2):
                        c = 2 * blk + j
                        rrk = [K_("RR", j)]
                        igk = [K_("IG", j)]
                        OP("dve", "scalar_tensor_tensor", IG[:, j, 0:sz], IG[:, j, 0:sz], 1.0, XC[:, j, 0:sz], op0=ALU.add, op1=ALU.mult,
                           reads=igk + [K_("XC", j)], writes=igk)
                        OP("dve", "tensor_tensor", IG[:, j, 0:sz], IG[:, j, 0:sz], RR[:, j, 0:sz], op=ALU.mult, reads=rrk + igk, writes=igk)
                        if hs:
                            a0 = AA[:, j, sl:sl + NSC].rearrange("p (s t) -> p s t", t=ST)[:, :, 0]
                            u0 = IG[:, j, sl:sl + NSC].rearrange("p (s t) -> p s t", t=ST)[:, :, 0]
                            OP("dve", "tensor_tensor", TMP16[:, j, :], a0, PST[:, c, R_H:R_H + NS], op=ALU.mult,
                               reads=[K_("AA", j)] + PSTK, writes=[("TMP16", j)])
                            OP("dve", "tensor_tensor", u0, u0, TMP16[:, j, :], op=ALU.add, reads=igk + [("TMP16", j)], writes=igk)
                            OP("dve", "memset", a0, 0.0, reads=[("TMP16", j)], writes=[K_("AA", j)])
                        OP("dve", "tensor_tensor_scan", HH[:, j, 0:sz], AA[:, j, 0:sz], IG[:, j, 0:sz], HST[:, c:c + 1],
                           op0=ALU.mult, op1=ALU.add, reads=[K_("AA", j), ("HST", c)] + igk, writes=[K_("RR", j)])
                        if tp > 0:
                            OP("dve", "tensor_copy", HST[:, c:c + 1], HH[:, j, tp - 1:tp], reads=[K_("RR", j)], writes=[("HST", c)])
                            if lastp_here:
                                OP("dve", "tensor_copy", OST[:, c, O_RGH_P:O_RGH_P + 1], HH[:, j, tp - 1:tp], reads=[K_("RR", j)], writes=[("OST", c)])
                        if hs:
                            OP("dve", "tensor_copy", OST[:, c, O_RGH_S:O_RGH_S + NS],
                               HH[:, j, sl:sl + NSC].rearrange("p (s t) -> p s t", t=ST)[:, :, ST - 1], reads=[K_("RR", j)], writes=[("OST", c)])
                        OP("dve", "tensor_tensor", G[:, c, o:o + sz], HH[:, j, 0:sz], GG[:, j, 0:sz], op=ALU.mult,
                           reads=[K_("RR", j), K_("GG", j)], writes=[("G", c, n)])

                return fb, ba, bd

            backs = []
            if "l0mix" not in skip and dbg >= 11:
                for bs_ in ([0, 1, 2, 3],):
                    sl_ = {}
                    for i, blk in enumerate(bs_):
                        sl_[blk] = next_weight(pending=i)
                    for n in allN:
                        if n == len(NT) - 1 and pend:
                            pend.pop()()
                        for bi_, blk in enumerate(bs_):
                            if len(NT) > 1 and n == len(NT) - 2 and bi_ == 2 and pend and not sq_done["v"]:
                                prenorm_sq(len(NT) - 1, NT[-1][0], NT[-1][1])
                                sq_done["v"] = True
                            fb_, ba_, bd_ = rg_unit(blk, n, *sl_[blk])
                            if backs:
                                backs[0][0]()
                            fb_()
                            if backs:
                                backs.pop(0)[1]()
                            backs.append((ba_, bd_))
            def drain_backs():
                while backs:
                    ba_, bd_ = backs.pop(0)
                    ba_()
                    bd_()

            if "l0mix" not in skip and dbg >= 15:
                if len(NT) > 1:
                    pend = mixer_out(NT, R_NORM + 2 * 1 + 0, (R_NORM + 2 * 2 + 0) if "ffn0" not in skip else None,
                                     hooks={(0, 3): drain_backs})
                else:
                    drain_backs()
                    pend = mixer_out(NT, R_NORM + 2 * 1 + 0, (R_NORM + 2 * 2 + 0) if "ffn0" not in skip else None)
            drain_backs()
            pend = ffn(NT, 0, (R_NORM + 1) if "l1mix" not in skip else None, pend)

            def sc_unit(c, n, wv, wkeys):
                o, sz, tp, hs, sl, lastp_here = tile_info(n)
                bsx = unit_i[0] % 2
                B_ = SCS[bsx]
                unit_i[0] += 1
                BG, CV, CS, CC, CG = (B_[k] for k in ("BG", "CV", "CS", "CC", "CG"))
                K_ = lambda nm: (nm, bsx)
                banks = {}
                for nm, mo in (("cg", 128), ("v", 256), ("bg", 0)):
                    b = next_bank()
                    banks[nm] = b
                    mm_group(PS[:, b, 0:sz], [(wv[:, k, mo:mo + 128], HN[:, k, o:o + sz]) for k in range(KC)],
                             lambda i: wkeys + [("HN", i, n)], b)
                ACTF(CG[:, 0:sz], PS[:, banks["cg"], 0:sz], AF.Copy, [("PS", banks["cg"])], [K_("CG")])
                OP("dve", "tensor_tensor", CV[:, 2:2 + sz], CG[:, 0:sz], PS[:, banks["v"], 0:sz], op=ALU.mult,
                   reads=[K_("CG"), ("PS", banks["v"])], writes=[K_("CV")])
                ACTF(BG[:, 0:sz], PS[:, banks["bg"], 0:sz], AF.Copy, [("PS", banks["bg"])], [K_("BG")])
                cvk = [K_("CV")]
                OP("dve", "tensor_copy", CV[:, 0:2], SCST[:, c, 0:2], reads=[("SCST", c)], writes=[K_("CVpre")])
                if tp > 0:
                    ACTF(CC[:, 0:tp], CV[:, 2:2 + tp], AF.Identity, cvk + PSTK, [K_("CC")], scale=pcol(c, R_SCW + 2))
                    for k in (1, 0):
                        OP("dve", "scalar_tensor_tensor", CC[:, 0:tp], CV[:, k:k + tp], pcol(c, R_SCW + k), CC[:, 0:tp],
                           op0=ALU.mult, op1=ALU.add, reads=cvk + [K_("CVpre"), K_("CC")] + PSTK, writes=[K_("CC")])
                    OP("dve", "tensor_copy", SCST[:, c, 0:2], CV[:, tp:tp + 2], reads=cvk + [K_("CVpre")], writes=[("SCST", c)])
                    if lastp_here:
                        OP("dve", "tensor_copy", OST[:, c, O_SC_P:O_SC_P + 2], CV[:, tp:tp + 2], reads=cvk + [K_("CVpre")], writes=[("OST", c)])
                if hs:
                    OP("dve", "tensor_copy", CS[:, :, 0:2], PST[:, c, R_SC:R_SC + 32].rearrange("p (s t) -> p s t", t=2),
                       reads=PSTK, writes=[K_("CS")])
                    OP("dve", "tensor_copy", CS[:, :, 2:6], CV[:, 2 + sl:2 + sl + NSC].rearrange("p (s t) -> p s t", t=ST),
                       reads=cvk + [K_("CS")], writes=[K_("CS")])
                    ccs = CC[:, sl:sl + NSC].rearrange("p (s t) -> p s t", t=ST)
                    OP("dve", "tensor_scalar", ccs, CS[:, :, 2:6], pcol(c, R_SCW + 2), None, op0=ALU.mult,
                       reads=[K_("CS")] + PSTK, writes=[K_("CC")])
                    for k in (1, 0):
                        OP("dve", "scalar_tensor_tensor", ccs, CS[:, :, k:k + ST], pcol(c, R_SCW + k), ccs,
                           op0=ALU.mult, op1=ALU.add, reads=[K_("CS"), K_("CC")] + PSTK, writes=[K_("CC")])
                    OP("dve", "tensor_copy", OST[:, c, O_SC_S:O_SC_S + 32].rearrange("p (s t) -> p s t", t=2), CS[:, :, 4:6],
                       reads=[K_("CS")], writes=[("OST", c)])
                OP("dve", "tensor_tensor", G[:, c, o:o + sz], BG[:, 0:sz], CC[:, 0:sz], op=ALU.mult,
                   reads=[K_("BG"), K_("CC")], writes=[("G", c, n)])

            if "l1mix" not in skip:
                for cs_ in ([0, 1, 2, 3], [4], [5], [6], [7]):
                    sl_ = {}
                    for i, c in enumerate(cs_):
                        sl_[c] = next_weight(pending=i)
                    for n in allN:
                        if n == len(NT) - 1 and pend:
                            pend.pop()()
                        for c in cs_:
                            sc_unit(c, n, *sl_[c])
            if "l1mix" not in skip:
                pend = mixer_out(NT, R_NORM + 2 * 1 + 1, (R_NORM + 2 * 2 + 1) if "ffn1" not in skip else None)
            hk, aa = group_tail_hooks(g)
            pend = ffn(NT, 1, None, pend, hk, aa)
            assert not pend

        if dbg >= 6:
            store_rows(lambda c: OST[:, c, 0:NOUTST], NOUTST, T, lambda c: [("OST", c)])
        S.final_wait("sp")

        @block.tensor
        def _(e):
            S.emit("pe", e, esem, dsems)

        @block.scalar
        def _(e):
            S.emit("act", e, esem, dsems)

        @block.vector
        def _(e):
            S.emit("dve", e, esem, dsems)

        @block.gpsimd
        def _(e):
            S.emit("pool", e, esem, dsems)

        @block.sync
        def _(e):
            S.emit("sp", e, esem, dsems)
    return nc


def make_core_inputs(c, x_prompt, x_sample, state_rglru_conv, state_rglru_h, state_sconv, meta_tokens,
                     norm_mix_pre, norm_mix_post, norm_ffn_pre, norm_ffn_post, rg_conv_w, rg_conv_b,
                     rg_gate_a_b, rg_gate_x_b, rg_lambda, sc_conv_w):
    f = lambda a: np.asarray(a, dtype=np.float32)
    s0, s1 = NS * c, NS * (c + 1)
    xin = np.concatenate([f(meta_tokens), f(x_prompt[c]), f(x_sample[s0:s1]).reshape(NSC, D)], axis=0)
    pst = np.concatenate([
        f(state_rglru_conv[0, s0:s1]).reshape(NS * 3, D),
        f(state_rglru_h[0, s0:s1]).reshape(NS, D),
        f(state_sconv[0, s0:s1]).reshape(NS * 2, D),
        f(norm_mix_pre), f(norm_mix_post), f(norm_ffn_pre), f(norm_ffn_post),
        f(rg_conv_w[0]), f(rg_conv_b), f(rg_gate_a_b).reshape(1, D), f(rg_gate_x_b).reshape(1, D),
        f(rg_lambda), f(sc_conv_w[0]),
    ], axis=0)
    assert pst.shape == (NPST, D), pst.shape
    return np.ascontiguousarray(xin), np.ascontiguousarray(pst)


_NC_CACHE = {}


def kernel(x_prompt, x_sample, state_rglru_conv, state_rglru_h, state_sconv, meta_tokens,
           norm_mix_pre, norm_mix_post, norm_ffn_pre, norm_ffn_post,
           rg_w_in, rg_conv_w, rg_conv_b, rg_gate_a_w, rg_gate_a_b, rg_gate_x_w, rg_gate_x_b,
           rg_lambda, rg_w_out, sc_w_in, sc_conv_w, sc_w_out,
           ffn_w_gate, ffn_w_up, ffn_w_down):
    f = lambda a: np.ascontiguousarray(np.asarray(a, dtype=np.float32))
    B, SEQ, _ = x_prompt.shape
    TP = NMETA + SEQ
    T = TP + NSC
    assert B == N_CORES and x_sample.shape[0] == NS * N_CORES and x_sample.shape[1] == ST
    if TP not in _NC_CACHE:
        _NC_CACHE[TP] = build(TP)
    nc = _NC_CACHE[TP]
    shared = {
        "ident": np.eye(128, dtype=np.float32),
        "rg_w_in": f(rg_w_in[0]), "ga_w": f(rg_gate_a_w[0]), "gx_w": f(rg_gate_x_w[0]),
        "rg_w_out": f(rg_w_out[0]), "sc_w_in": f(sc_w_in[0]), "sc_w_out": f(sc_w_out[0]),
        "w_gate": f(ffn_w_gate), "w_up": f(ffn_w_up), "w_down": f(ffn_w_down),
    }
    in_maps = []
    for c in range(N_CORES):
        xin, pst = make_core_inputs(c, x_prompt, x_sample, state_rglru_conv, state_rglru_h, state_sconv, meta_tokens,
                                    norm_mix_pre, norm_mix_post, norm_ffn_pre, norm_ffn_post, rg_conv_w, rg_conv_b,
                                    rg_gate_a_b, rg_gate_x_b, rg_lambda, sc_conv_w)
        m = dict(shared)
        m["xin"] = xin
        m["pst"] = pst
        in_maps.append(m)
    res = run_bass_kernel_spmd(nc, in_maps, core_ids=list(range(N_CORES)))
    outs = [np.asarray(r["yout"], dtype=np.float32) for r in res.results]
    y_prompt = np.stack([o[NMETA:TP] for o in outs], axis=0)
    y_sample = np.concatenate([o[TP:T].reshape(NS, ST, D) for o in outs], axis=0)
    st = [o[T:T + NOUTST] for o in outs]
    rg_conv_p = np.stack([s[O_RGC_P:O_RGC_P + 3] for s in st], axis=0)[None]
    rg_h_p = np.stack([s[O_RGH_P] for s in st], axis=0)[None]
    sc_p = np.stack([s[O_SC_P:O_SC_P + 2] for s in st], axis=0)[None]
    rg_conv_s = np.concatenate([s[O_RGC_S:O_RGC_S + 48].reshape(NS, 3, D) for s in st], axis=0)[None]
    rg_h_s = np.concatenate([s[O_RGH_S:O_RGH_S + NS] for s in st], axis=0)[None]
    sc_s = np.concatenate([s[O_SC_S:O_SC_S + 32].reshape(NS, 2, D) for s in st], axis=0)[None]
    c = np.ascontiguousarray
    return (c(y_prompt), c(y_sample), c(rg_conv_p), c(rg_h_p), c(sc_p), c(rg_conv_s), c(rg_h_s), c(sc_s))
```
